# Optimizing a Trainium2 kernel written in Bass

```python
import jax, jax.numpy as jnp
from jax import lax
import numpy as np

D_MODEL = 2048
BATCH = 4
SEQ = 2048
DEPTH = 1
DEC_BATCH = 128
DEC_SEQ = 4
PAST_LEN = 16384
PAGE_SIZE = 128

MIX_WIDTH = D_MODEL
W_A = MIX_WIDTH // 2
W_B = MIX_WIDTH - W_A
CHUNK = 128
H_A = 8
DA = W_A // H_A
HEAD_B = 64
H_B = W_B // HEAD_B
LORA_W = 64
LORA_A = 64
LORA_G = 160
W_RWKV_IN = 3 * W_B + LORA_W + LORA_A + LORA_G
W_IN = 2 * W_A + W_RWKV_IN
N_KEYS = 128
N_EXPERTS = N_KEYS * N_KEYS
PEER_HEADS = 8
D_KEY = 256
HALF_KEY = D_KEY // 2
TOPK = 16
TOK_BLOCK = 128
RMS_EPS = 1e-6
LN_EPS = 1e-5
GN_EPS = 6.4e-4

kernel_name = 'hymba_gmlp_rwkv7_peer_step'


def rms_norm(x, g):
    xf = x.astype(jnp.float32)
    y = xf * lax.rsqrt(jnp.mean(xf * xf, axis=-1, keepdims=True) + RMS_EPS)
    return (y * g.astype(jnp.float32)).astype(x.dtype)


def layer_norm(x, g, b, eps):
    xf = x.astype(jnp.float32)
    mu = jnp.mean(xf, axis=-1, keepdims=True)
    xc = xf - mu
    var = jnp.mean(xc * xc, axis=-1, keepdims=True)
    return xc * lax.rsqrt(var + eps) * g.astype(jnp.float32) + b.astype(jnp.float32)


def chunk_spatial_gate(z, ln_g, ln_b, ws, bs):
    bn, t_len, _ = z.shape
    u, v = z[..., :W_A], z[..., W_A:]
    v = layer_norm(v, ln_g, ln_b, LN_EPS).astype(z.dtype)
    n_chunks = -(-t_len // CHUNK)
    pad = n_chunks * CHUNK - t_len
    vp = jnp.pad(v, ((0, 0), (0, pad), (0, 0))).reshape(bn, n_chunks, CHUNK, H_A, DA)
    mask = jnp.tril(jnp.ones((CHUNK, CHUNK), dtype=bool))
    ws_c = jnp.where(mask[None], ws, 0)
    mixed = jnp.einsum('hts,bcshd->bcthd', ws_c, vp) + bs.T[None, None, :, :, None]
    mixed = mixed.reshape(bn, n_chunks * CHUNK, W_A)[:, :t_len]
    start = ((t_len - 1) // CHUNK) * CHUNK
    return u * mixed, v[:, start:]


def rwkv7_mix(p, p_prev_row, s0, mu, w0, w_up, a0, a_up, g_up, k_k, k_a, r_k, gn_g, gn_b):
    f32 = jnp.float32
    bn, t_len, _ = p.shape
    pf = p.astype(f32)
    p_prev = jnp.concatenate([p_prev_row.astype(f32)[:, None], pf[:, :-1]], axis=1)
    q = pf + (p_prev - pf) * mu.astype(f32)
    r = q[..., :W_B]
    k = q[..., W_B:2 * W_B]
    v = q[..., 2 * W_B:3 * W_B]
    o = 3 * W_B
    xw = q[..., o:o + LORA_W]
    xa = q[..., o + LORA_W:o + LORA_W + LORA_A]
    xg = q[..., o + LORA_W + LORA_A:]
    w = -jax.nn.softplus(-(w0.astype(f32) + jnp.tanh(xw) @ w_up.astype(f32))) - 0.5
    decay = jnp.exp(-jnp.exp(w))
    a = jax.nn.sigmoid(a0.astype(f32) + xa @ a_up.astype(f32))
    g = jax.nn.sigmoid(xg) @ g_up.astype(f32)
    hs = (bn, t_len, H_B, HEAD_B)
    kk = (k * k_k.astype(f32)).reshape(hs)
    kk = kk * lax.rsqrt(jnp.maximum(jnp.sum(kk * kk, axis=-1, keepdims=True), 1e-24))
    k = k * (1.0 + (a - 1.0) * k_a.astype(f32))
    r_h, k_h, v_h = r.reshape(hs), k.reshape(hs), v.reshape(hs)
    a_h, d_h = a.reshape(hs), decay.reshape(hs)

    def step(state, inp):
        r_t, w_t, k_t, v_t, a_t, b_t = inp
        sa = jnp.einsum('bhij,bhj->bhi', state, a_t)
        state = (state * w_t[:, :, None, :] + sa[..., None] * b_t[:, :, None, :]
                 + v_t[..., None] * k_t[:, :, None, :])
        y_t = jnp.einsum('bhij,bhj->bhi', state, r_t)
        return state, y_t

    xs = tuple(jnp.moveaxis(t, 1, 0) for t in (r_h, d_h, k_h, v_h, -kk, kk * a_h))
    s_final, ys = lax.scan(step, s0.astype(f32), xs)
    y = jnp.moveaxis(ys, 0, 1)
    y = layer_norm(y, gn_g.reshape(H_B, HEAD_B), gn_b.reshape(H_B, HEAD_B), GN_EPS)
    bonus = jnp.sum(r_h * k_h * r_k.astype(f32), axis=-1, keepdims=True) * v_h
    out = (y + bonus).reshape(bn, t_len, W_B) * g
    return out.astype(p.dtype), s_final


def peer_ffn(xn, w_q, sub_keys, u_tab, v_tab):
    f32 = jnp.float32
    bn, t_len, d = xn.shape
    n_tok = bn * t_len
    xt = xn.reshape(n_tok, d)
    q = (xt @ w_q).reshape(n_tok, PEER_HEADS, 2, HALF_KEY).astype(f32)
    s = jnp.einsum('nhcd,ckd->nhck', q, sub_keys.astype(f32))
    top_s, top_i = lax.top_k(s, TOPK)
    cand_s = top_s[..., 0, :, None] + top_s[..., 1, None, :]
    cand_i = top_i[..., 0, :, None] * N_KEYS + top_i[..., 1, None, :]
    best_s, best_j = lax.top_k(cand_s.reshape(n_tok, PEER_HEADS, TOPK * TOPK), TOPK)
    experts = jnp.take_along_axis(cand_i.reshape(n_tok, PEER_HEADS, TOPK * TOPK), best_j, axis=-1)
    gates = jax.nn.softmax(best_s, axis=-1)
    n_blk = -(-n_tok // TOK_BLOCK)
    pad = n_blk * TOK_BLOCK - n_tok
    xb = jnp.pad(xt, ((0, pad), (0, 0))).reshape(n_blk, TOK_BLOCK, d)
    eb = jnp.pad(experts, ((0, pad), (0, 0), (0, 0))).reshape(n_blk, TOK_BLOCK, PEER_HEADS, TOPK)
    gb = jnp.pad(gates, ((0, pad), (0, 0), (0, 0))).reshape(n_blk, TOK_BLOCK, PEER_HEADS, TOPK)

    def block(args):
        x_b, e_b, g_b = args
        ue = u_tab[e_b]
        act = jax.nn.gelu(jnp.einsum('nhkd,nd->nhk', ue, x_b).astype(f32), approximate=False)
        coef = (g_b * act).astype(x_b.dtype)
        return jnp.einsum('nhk,nhkd->nd', coef, v_tab[e_b])

    out = lax.map(block, (xb, eb, gb)).reshape(n_blk * TOK_BLOCK, d)[:n_tok]
    return out.reshape(bn, t_len, d)


def _layer(x, shift_row, s0, norm1_g, w_in, ln_v_g, ln_v_b, ws, bs, mu, w0, w_up, a0, a_up,
           g_up, k_k, k_a, r_k, gn_g, gn_b, w_out, norm2_g, w_q, sub_keys, u_tab, v_tab):
    xn = rms_norm(x, norm1_g)
    proj = xn @ w_in
    z_a = jax.nn.gelu(proj[..., :2 * W_A], approximate=False)
    p_b = proj[..., 2 * W_A:]
    prev_row = shift_row.astype(xn.dtype) @ w_in[:, 2 * W_A:]
    y_a, v_rows = chunk_spatial_gate(z_a, ln_v_g, ln_v_b, ws, bs)
    y_b, s_new = rwkv7_mix(p_b, prev_row, s0, mu, w0, w_up, a0, a_up, g_up, k_k, k_a, r_k, gn_g, gn_b)
    h = x + jnp.concatenate([y_a, y_b], axis=-1) @ w_out
    y = h + peer_ffn(rms_norm(h, norm2_g), w_q, sub_keys, u_tab, v_tab)
    return y, s_new, xn[:, -1], v_rows


def setup_inputs(seed: int = 0) -> dict:
    key = jax.random.key(seed)
    ks = jax.random.split(key, 28)
    f32 = jnp.float32

    def nrm(k, shape, s):
        return s * jax.random.normal(k, shape, f32)

    L = DEPTH
    return {
        'x_prompt': nrm(ks[0], (BATCH, SEQ, D_MODEL), 1.0),
        'x_sample': nrm(ks[1], (DEC_BATCH, DEC_SEQ, D_MODEL), 1.0),
        'state_wkv': nrm(ks[2], (L, DEC_BATCH, H_B, HEAD_B, HEAD_B), 0.3),
        'state_shift': nrm(ks[3], (L, DEC_BATCH, D_MODEL), 1.0),
        'norm1_g': 1.0 + nrm(ks[4], (L, D_MODEL), 0.02),
        'w_in': nrm(ks[5], (L, D_MODEL, W_IN), D_MODEL ** -0.5),
        'ln_v_g': 1.0 + nrm(ks[6], (L, W_A), 0.02),
        'ln_v_b': nrm(ks[7], (L, W_A), 0.02),
        'ws': nrm(ks[8], (L, H_A, CHUNK, CHUNK), 0.5 * CHUNK ** -0.5),
        'bs': 1.0 + nrm(ks[9], (L, H_A, CHUNK), 0.1),
        'mu': jax.random.uniform(ks[10], (L, W_RWKV_IN), f32),
        'w0': -1.0 + nrm(ks[11], (L, W_B), 0.5),
        'w_up': nrm(ks[12], (L, LORA_W, W_B), 0.1),
        'a0': nrm(ks[13], (L, W_B), 0.1),
        'a_up': nrm(ks[14], (L, LORA_A, W_B), 0.1),
        'g_up': nrm(ks[15], (L, LORA_G, W_B), LORA_G ** -0.5),
        'k_k': 0.85 + nrm(ks[16], (L, W_B), 0.05),
        'k_a': 1.0 + nrm(ks[17], (L, W_B), 0.05),
        'r_k': nrm(ks[18], (L, H_B, HEAD_B), 0.1),
        'gn_g': 1.0 + nrm(ks[19], (L, W_B), 0.02),
        'gn_b': nrm(ks[20], (L, W_B), 0.02),
        'w_out': nrm(ks[21], (L, W_A + W_B, D_MODEL), (W_A + W_B) ** -0.5),
        'norm2_g': 1.0 + nrm(ks[22], (L, D_MODEL), 0.02),
        'w_q': nrm(ks[23], (L, D_MODEL, PEER_HEADS * D_KEY), D_MODEL ** -0.5),
        'sub_keys': nrm(ks[24], (L, 2, N_KEYS, HALF_KEY), HALF_KEY ** -0.5),
        'u_tab': nrm(ks[25], (L, N_EXPERTS, D_MODEL), D_MODEL ** -0.5),
        'v_tab': nrm(ks[26], (L, N_EXPERTS, D_MODEL), 0.1),
        'final_g': 1.0 + nrm(ks[27], (D_MODEL,), 0.02),
    }


def reference(x_prompt, x_sample, state_wkv, state_shift, norm1_g, w_in, ln_v_g, ln_v_b, ws, bs,
              mu, w0, w_up, a0, a_up, g_up, k_k, k_a, r_k, gn_g, gn_b, w_out, norm2_g, w_q,
              sub_keys, u_tab, v_tab, final_g):
    bp = x_prompt.shape[0]
    shift0 = jnp.zeros((bp, D_MODEL), x_prompt.dtype)
    wkv0 = jnp.zeros((bp, H_B, HEAD_B, HEAD_B), jnp.float32)
    yp, ys = x_prompt, x_sample
    p_wkv, p_shift, p_chunk, s_wkv, s_shift, s_chunk = [], [], [], [], [], []
    for l in range(DEPTH):
        lw = [w[l] for w in (norm1_g, w_in, ln_v_g, ln_v_b, ws, bs, mu, w0, w_up, a0, a_up, g_up,
                             k_k, k_a, r_k, gn_g, gn_b, w_out, norm2_g, w_q, sub_keys, u_tab, v_tab)]
        yp, wkv_p, sh_p, cv_p = _layer(yp, shift0, wkv0, *lw)
        ys, wkv_s, sh_s, cv_s = _layer(ys, state_shift[l], state_wkv[l], *lw)
        p_wkv.append(wkv_p); p_shift.append(sh_p); p_chunk.append(cv_p)
        s_wkv.append(wkv_s); s_shift.append(sh_s); s_chunk.append(cv_s)
    y_prompt = rms_norm(yp, final_g)
    y_sample = rms_norm(ys, final_g)
    return (y_prompt, y_sample, jnp.stack(p_wkv), jnp.stack(p_shift), jnp.stack(p_chunk),
            jnp.stack(s_wkv), jnp.stack(s_shift), jnp.stack(s_chunk))
```

```python
import math, os
from contextlib import ExitStack
import numpy as np
import concourse.bass as bass
import concourse.mybir as mybir
from concourse.bass_utils import run_bass_kernel_spmd

F32 = mybir.dt.float32
BF16 = mybir.dt.bfloat16
I32 = mybir.dt.int32
U32 = mybir.dt.uint32
AF = mybir.ActivationFunctionType
ALU = mybir.AluOpType
AX = mybir.AxisListType

D = 2048
WIN = 5408
WR = 3360
NT = 2048
NOWN = 1024
NS = 64
NEXP = 16384
RMS_EPS = 1e-6
LN_EPS = 1e-5
GN_EPS = 6.4e-4


class Sched:
    def __init__(self, nc, es):
        self.nc = nc
        self.eng = {'pe': nc.tensor, 'dve': nc.vector, 'act': nc.scalar, 'pool': nc.gpsimd, 'sp': nc.sync}
        self.sem = {k: es.enter_context(nc.semaphore('s_' + k)) for k in self.eng}
        self.cnt = {k: 0 for k in self.eng}
        self.NDS = 96
        self.dsem = [es.enter_context(nc.semaphore('d%d' % i)) for i in range(self.NDS)]
        self.dcnt = [0] * self.NDS
        self.dnext = 0
        self.dnext_sw = 0
        self.NHW = 72
        self.waited = {k: {} for k in self.eng}
        self.lastw = {}
        self.readers = {}
        self.rec = None

    def _wait(self, e, tok):
        if tok is None:
            return
        key, val = tok
        if self.waited[e].get(key, 0) >= val:
            return
        self.waited[e][key] = val
        sem = self.sem[key] if isinstance(key, str) else self.dsem[key]
        self.eng[e].wait_ge(sem, val)

    def _deps(self, e, reads, writes):
        for r in reads:
            self._wait(e, self.lastw.get(r))
        for w in writes:
            self._wait(e, self.lastw.get(w))
            for tok in self.readers.get(w, ()):
                self._wait(e, tok)

    def _commit(self, tok, reads, writes):
        for r in reads:
            if r in self.lastw:
                self.readers.setdefault(r, []).append(tok)
        for w in writes:
            self.lastw[w] = tok
            self.readers[w] = []

    def op(self, e, fn, reads=(), writes=()):
        if self.rec is not None:
            self.rec.append(('op', e, fn, tuple(reads), tuple(writes)))
            return
        self._deps(e, reads, writes)
        ins = fn(self.eng[e])
        self.cnt[e] += 1
        ins.then_inc(self.sem[e], 1)
        self._commit((e, self.cnt[e]), reads, writes)

    def _dtok(self, sw=False):
        if sw:
            j = self.NHW + self.dnext_sw
            self.dnext_sw = (self.dnext_sw + 1) % (self.NDS - self.NHW)
        else:
            j = self.dnext
            self.dnext = (self.dnext + 1) % self.NHW
        self.dcnt[j] += 16
        return j

    def dma(self, q, out, in_, reads=(), writes=(), **kw):
        if self.rec is not None:
            self.rec.append(('dma', q, out, in_, tuple(reads), tuple(writes), kw))
            return
        self._deps(q, reads, writes)
        j = self._dtok(sw=(q == 'pool'))
        if self.dcnt[j] > 16:
            self._wait(q, (j, self.dcnt[j] - 16))
        self.eng[q].dma_start(out=out, in_=in_, **kw).then_inc(self.dsem[j], 16)
        self._commit((j, self.dcnt[j]), reads, writes)

    def gather(self, out, table, idx, nrows, reads=(), writes=()):
        self._deps('pool', reads, writes)
        j = self._dtok(sw=True)
        if self.dcnt[j] > 16:
            self._wait('pool', (j, self.dcnt[j] - 16))
        if getattr(self, 'breg', None) is None:
            self.breg = self.nc.gpsimd.to_reg(nrows - 1)
        self.nc.gpsimd.indirect_dma_start(
            out=out, out_offset=None, in_=table,
            in_offset=bass.IndirectOffsetOnAxis(ap=idx, axis=0),
            bounds_check=self.breg, oob_is_err=False).then_inc(self.dsem[j], 16)
        self._commit((j, self.dcnt[j]), reads, writes)

    def replay(self, item):
        if item[0] == 'op':
            self.op(item[1], item[2], item[3], item[4])
        else:
            self.dma(item[1], item[2], item[3], item[4], item[5], **item[6])

    def barrier(self):
        for e in self.eng:
            for k in self.eng:
                if self.cnt[k] > 0:
                    self._wait(e, (k, self.cnt[k]))
            for j in range(self.NDS):
                if self.dcnt[j] > 0:
                    self._wait(e, (j, self.dcnt[j]))

    def finish(self, e='sp'):
        for k in list(self.lastw):
            self._wait(e, self.lastw[k])


def bc_rows(ap1d, n, parts=128):
    return bass.AP(tensor=ap1d.tensor, offset=ap1d.offset, ap=[[0, parts], [1, n]])


def build(debug=False):
    nc = bass.Bass("TRN2", target_bir_lowering=False)

    def din(name, shape, dt=F32):
        return nc.dram_tensor(name, list(shape), dt, kind="ExternalInput").ap()

    def dout(name, shape, dt=F32):
        return nc.dram_tensor(name, list(shape), dt, kind="ExternalOutput").ap()

    def dscr(name, shape, dt=F32):
        if debug:
            return nc.dram_tensor(name, list(shape), dt, kind="ExternalOutput").ap()
        return nc.dram_tensor(name, list(shape), dt).ap()

    xseq = din("xseq", [NT, D])
    xs = din("xs", [NS, D])
    wkv0 = din("wkv0", [128, 8192])
    shift0 = din("shift0", [16, D])
    norm1_g = din("norm1_g", [D]); norm2_g = din("norm2_g", [D]); final_g = din("final_g", [D])
    w_in = din("w_in", [D, WIN]); w_out = din("w_out", [D, D]); w_q = din("w_q", [D, D])
    ln_g = din("ln_g", [1024]); ln_b = din("ln_b", [1024])
    wsT = din("wsT", [128, 8, 128]); bsT = din("bsT", [128, 8])
    wsTs = din("wsTs", [128, 8, 128]); bsTs = din("bsTs", [128, 8])
    mu = din("mu", [WR]); w0 = din("w0", [1024]); a0 = din("a0", [1024])
    wa_up = din("wa_up", [128, 1024]); g_up = din("g_up", [160, 1024])
    k_k = din("k_k", [1024]); k_a = din("k_a", [1024]); r_k = din("r_k", [1024])
    gn_g = din("gn_g", [1024]); gn_b = din("gn_b", [1024])
    skT = din("skT", [128, 2, 128])
    u_tab = din("u_tab", [NEXP, D]); v_tab = din("v_tab", [NEXP, D])

    y_p = dout("y_p", [NOWN, D]); y_s = dout("y_s", [NS, D])
    wkv_p = dout("wkv_p", [128, 512]); shift_p = dout("shift_p", [1, D]); chunkv_p = dout("chunkv_p", [128, 1024])
    wkv_s = dout("wkv_s", [128, 8192]); shift_s = dout("shift_s", [16, D]); chunkv_s = dout("chunkv_s", [NS, 1024])

    projP = dscr("projP", [NT + 1, WIN])
    projS = dscr("projS", [80, WIN])
    QN = ["r", "w", "k", "a", "b"]
    scP = {q: dscr("scP_" + q, [NT, 1024]) for q in QN + ["v"]}
    scS = {q: dscr("scS_" + q, [NS, 1024]) for q in QN + ["v"]}
    yscP = dscr("yscP", [NT, 1024]); yscS = dscr("yscS", [NS, 1024])
    gP = dscr("gP", [NOWN + NS, 1024]); bonP = dscr("bonP", [NOWN + NS, 1024]); yaP = dscr("yaP", [NOWN + NS, 1024])

    with ExitStack() as es:
        S = Sched(nc, es)

        _nm = {'n': 0}

        def sbuf(st, name, shape, dt=F32):
            _nm['n'] += 1
            return st.enter_context(nc.sbuf_tensor("%s_%d" % (name, _nm['n']), list(shape), dt))

        PS = es.enter_context(nc.psum_tensor("PS", [128, 8, 512], F32))
        io_fp = sbuf(es, "io_fp", [128, 128])
        io_f = sbuf(es, "io_f", [128, 256])
        ident = sbuf(es, "ident", [128, 128])
        identb = sbuf(es, "identb", [128, 128], BF16)
        cmask = sbuf(es, "cmask", [128, 128])
        zero_t = sbuf(es, "zero_t", [128, 512])
        S.op('pool', lambda e: e.iota(io_fp[:], pattern=[[1, 128]], base=0, channel_multiplier=-1,
                                      allow_small_or_imprecise_dtypes=True), writes=['io_fp'])
        S.op('pool', lambda e: e.iota(io_f[:], pattern=[[1, 256]], base=0, channel_multiplier=0,
                                      allow_small_or_imprecise_dtypes=True), writes=['io_f'])
        S.op('dve', lambda e: e.tensor_scalar(out=ident[:], in0=io_fp[:], scalar1=0.0, scalar2=None, op0=ALU.is_equal),
             reads=['io_fp'], writes=['ident'])
        S.op('dve', lambda e: e.tensor_copy(out=identb[:], in_=ident[:]), reads=['ident'], writes=['identb'])
        S.op('dve', lambda e: e.tensor_scalar(out=cmask[:], in0=io_fp[:], scalar1=0.0, scalar2=None, op0=ALU.is_ge),
             reads=['io_fp'], writes=['cmask'])
        S.op('dve', lambda e: e.memset(zero_t[:], 0.0), writes=['zero_t'])

        rr = {'ev': 0}

        def evac_eng():
            rr['ev'] += 1
            return 'act' if rr['ev'] % 2 else 'dve'

        def copy(e, out, in_, reads, writes):
            if e == 'act':
                S.op('act', lambda g: g.activation(out=out, in_=in_, func=AF.Copy), reads=reads, writes=writes)
            else:
                S.op(e, lambda g: g.tensor_copy(out=out, in_=in_), reads=reads, writes=writes)

        def rstd_from_ss(ss, n, eps, tag):
            S.op('act', lambda g: g.activation(out=ss, in_=ss, func=AF.Sqrt, bias=eps_tiles[eps][:, 0:1], scale=1.0 / n),
                 reads=[tag, 'epsc'], writes=[tag])
            S.op('dve', lambda g: g.reciprocal(out=ss, in_=ss), reads=[tag], writes=[tag])

        eps_tiles = {}
        for ev in (RMS_EPS, LN_EPS, GN_EPS, 0.0):
            t = sbuf(es, "eps%d" % len(eps_tiles), [128, 1])
            S.op('dve', lambda e, t=t, ev=ev: e.memset(t[:], ev), writes=['epsc'])
            eps_tiles[ev] = t

        def transpose_to(dst_fn, src_tile, nblk, skey, dkey, psb=(0, 1)):
            for g0 in range(0, nblk, 4):
                nb = min(4, nblk - g0)
                bank = psb[(g0 // 4) % len(psb)]
                for i in range(nb):
                    c = g0 + i
                    S.op('pe', lambda e, c=c, i=i, bank=bank: e.transpose(out=PS[:, bank, i * 128:(i + 1) * 128],
                                                                    in_=src_tile[:, c * 128:(c + 1) * 128], identity=ident[:]),
                         reads=[skey, 'ident'], writes=['ps%d' % bank])
                copy(evac_eng(), dst_fn(g0, nb), PS[:, bank, 0:nb * 128].rearrange("p (a b) -> p a b", a=nb),
                     ['ps%d' % bank], [dkey])

        def rmsnorm(xt, gt, xkey, gkey, ss, junk, outt=None, okey=None, jkey='junk'):
            outt = xt if outt is None else outt
            okey = xkey if okey is None else okey
            S.op('dve', lambda e: e.memset(ss[:, 0:1], 0.0), writes=['ss'])
            S.op('act', lambda e: e.activation(out=junk[:], in_=xt[:], func=AF.Square, accum_out=ss[:, 0:1]),
                 reads=[xkey, 'ss'], writes=[jkey, 'ss'])
            rstd_from_ss(ss[:, 0:1], D, RMS_EPS, 'ss')
            S.op('dve', lambda e: e.scalar_tensor_tensor(out=outt[:], in0=xt[:], scalar=ss[:, 0:1], in1=gt[:],
                                                         op0=ALU.mult, op1=ALU.mult),
                 reads=[xkey, 'ss', gkey], writes=[okey])

        NTT = 17
        with ExitStack() as p1:
            xnT = sbuf(p1, "xnT", [128, 16, NTT * 128], BF16)
            g1 = sbuf(p1, "g1", [128, D])
            xt2 = [sbuf(p1, "xt%d" % i, [128, D]) for i in range(2)]
            junk = sbuf(p1, "junk", [128, D])
            ss = sbuf(p1, "ss", [128, 4])
            wb = [sbuf(p1, "wb%d" % i, [128, 16, 512], BF16) for i in range(2)]
            stg = [sbuf(p1, "stg%d" % i, [128, 512]) for i in range(4)]
            S.dma('sp', g1[:], bc_rows(norm1_g, D), writes=['g1'])
            S.dma('sp', projP[0:1, 0:5120].rearrange("o (a b) -> (o a) b", a=10), zero_t[0:10, :], reads=['zero_t'], writes=['projP_z'])
            S.dma('sp', projP[0:1, 5120:WIN], zero_t[0:1, 0:WIN - 5120], reads=['zero_t'], writes=['projP_z'])
            for tt in range(NTT):
                xt = xt2[tt % 2]
                xk = 'xt%d' % (tt % 2)
                if tt < 16:
                    S.dma('sp', xt[:], xseq[tt * 128:(tt + 1) * 128, :], writes=[xk])
                else:
                    S.op('pool', lambda e: e.memset(xt[:], 0.0), writes=[xk])
                    S.dma('sp', xt[0:NS, :], xs[:, :], writes=[xk])
                rmsnorm(xt, g1, xk, 'g1', ss, junk)
                if tt == 15:
                    S.dma('sp', shift_p[0:1, :], xt[127:128, :], reads=[xk], writes=['o_shift_p'])
                if tt == 16:
                    S.dma('sp', shift_s[:, :], xt[48:64, :], reads=[xk], writes=['o_shift_s'])
                    S.dma('sp', xt[64:80, :], shift0[:, :], reads=[], writes=[xk])
                transpose_to(lambda c0, nb, tt=tt: xnT[:, c0:c0 + nb, tt * 128:(tt + 1) * 128], xt, 16, xk, 'xnT')
            ncb = (WIN + 511) // 512
            ei = 0
            for cb in range(ncb):
                c0 = cb * 512
                cw = min(512, WIN - c0)
                w = wb[cb % 2]
                wk = 'wb%d' % (cb % 2)
                S.dma('pool', w[:, :, 0:cw], w_in.rearrange("(a p) c -> p a c", p=128)[:, :, c0:c0 + cw], writes=[wk])
                for tt in range(NTT):
                    if tt < 8 and c0 < 2048:
                        continue
                    bank = 2 + (ei % 2)
                    for dc in range(16):
                        S.op('pe', lambda e, dc=dc, bank=bank, tt=tt, w=w, cw=cw: e.matmul(
                            PS[:, bank, 0:cw], lhsT=xnT[:, dc, tt * 128:(tt + 1) * 128], rhs=w[:, dc, 0:cw],
                            start=(dc == 0), stop=(dc == 15)), reads=['xnT', wk], writes=['ps%d' % bank])
                    st = stg[ei % 4]
                    sk = 'stg%d' % (ei % 4)
                    copy(evac_eng(), st[:, 0:cw], PS[:, bank, 0:cw], ['ps%d' % bank], [sk])
                    if tt < 16:
                        S.dma('sp', projP[1 + tt * 128:1 + (tt + 1) * 128, c0:c0 + cw], st[:, 0:cw], reads=[sk],
                              writes=['projP_%d_%d' % (tt, cb)])
                    else:
                        S.dma('sp', projS[16:80, c0:c0 + cw], st[0:64, 0:cw], reads=[sk], writes=['projS_a%d' % cb])
                        S.dma('sp', projS[0:16, c0:c0 + cw], st[64:80, 0:cw], reads=[sk], writes=['projS_b%d' % cb])
                    ei += 1
        S.barrier()
        proj_keys = lambda tt: (['projP_%d_%d' % (t2, cb) for t2 in (tt, tt - 1) if t2 >= 0 for cb in range(11)
                                 if not (t2 < 8 and cb < 4)] + ['projP_z']) if tt < 16 else \
            (['projS_a%d' % cb for cb in range(11)] + ['projS_b%d' % cb for cb in range(11)])

        with ExitStack() as p2:
            cst = {}
            for nm, src, n in (("mu", mu, WR), ("lng", ln_g, 1024), ("lnb", ln_b, 1024), ("w0", w0, 1024), ("a0", a0, 1024),
                               ("kk", k_k, 1024), ("ka", k_a, 1024), ("rk", r_k, 1024)):
                cst[nm] = sbuf(p2, "c_" + nm, [128, n])
                S.dma('sp', cst[nm][:], bc_rows(src, n), writes=['c_' + nm])
            waup = sbuf(p2, "waup", [128, 1024]); gup1 = sbuf(p2, "gup1", [128, 1024]); gup2 = sbuf(p2, "gup2", [32, 1024])
            S.dma('sp', waup[:], wa_up[:, :], writes=['waup'])
            S.dma('sp', gup1[:], g_up[0:128, :], writes=['gup1'])
            S.dma('sp', gup2[:], g_up[128:160, :], writes=['gup2'])
            wsf = sbuf(p2, "wsf", [128, 8, 128]); wsb = sbuf(p2, "wsb", [128, 8, 128], BF16); wsbs = sbuf(p2, "wsbs", [128, 8, 128], BF16)
            bst = sbuf(p2, "bst", [128, 8]); bsts = sbuf(p2, "bsts", [128, 8])
            S.dma('sp', bst[:], bsT[:, :], writes=['bst'])
            S.dma('sp', bsts[:], bsTs[:, :], writes=['bsts'])
            for srcw, dstw, key in ((wsT, wsb, 'wsb'), (wsTs, wsbs, 'wsbs')):
                S.dma('sp', wsf[:], srcw[:, :, :], writes=['wsf'])
                S.op('dve', lambda e, dstw=dstw: e.tensor_tensor(out=dstw[:], in0=wsf[:],
                                                                in1=cmask[:].unsqueeze(1).to_broadcast([128, 8, 128]), op=ALU.mult),
                     reads=['wsf', 'cmask'], writes=[key])
            pz = sbuf(p2, "pz", [128, 2048]); vb = sbuf(p2, "vb", [128, 1024], BF16)
            q = sbuf(p2, "q", [128, WR]); pp = sbuf(p2, "pp", [128, WR])
            ss = sbuf(p2, "ss", [128, 4]); st16 = sbuf(p2, "st16", [128, 16]); st16b = sbuf(p2, "st16b", [128, 16])
            lT1 = sbuf(p2, "lT1", [128, 128]); lT2 = sbuf(p2, "lT2", [128, 128]); lT3 = sbuf(p2, "lT3", [32, 128])
            W = sbuf(p2, "W", [128, 1024]); A = sbuf(p2, "A", [128, 1024]); KK = sbuf(p2, "KK", [128, 1024])
            K2 = sbuf(p2, "K2", [128, 1024]); AS = sbuf(p2, "AS", [128, 1024]); BS = sbuf(p2, "BS", [128, 1024])
            T = sbuf(p2, "T", [128, 1024]); G = sbuf(p2, "G", [128, 1024]); ya = sbuf(p2, "ya", [128, 1024])
            S.op('pool', lambda e: e.memset(pz[:], 0.0), writes=['pz'])
            S.op('pool', lambda e: e.memset(q[:], 0.0), writes=['q'])
            S.op('pool', lambda e: e.memset(pp[:], 0.0), writes=['pp'])
            v3 = lambda t: t[:].rearrange("p (h j) -> p h j", h=16)
            bc16 = lambda t: t[:, 0:16].unsqueeze(2).to_broadcast([128, 16, 64])
            for tt in range(NTT):
                own = tt >= 8
                samp = tt == 16
                nr = NS if samp else 128
                pk = proj_keys(tt)
                orow = (tt - 8) * 128
                if own:
                    if samp:
                        S.dma('sp', pz[0:NS, :], projS[16:80, 0:2048], reads=pk, writes=['pz'])
                    else:
                        S.dma('sp', pz[:], projP[1 + tt * 128:1 + (tt + 1) * 128, 0:2048], reads=pk, writes=['pz'])
                    S.op('act', lambda e: e.activation(out=pz[:], in_=pz[:], func=AF.Gelu), reads=['pz'], writes=['pz'])
                    vv = pz[:, 1024:2048]
                    S.op('dve', lambda e: e.tensor_reduce(out=ss[:, 0:1], in_=vv, axis=AX.X, op=ALU.add), reads=['pz'], writes=['ss'])
                    S.op('dve', lambda e: e.tensor_scalar(out=ss[:, 0:1], in0=ss[:, 0:1], scalar1=-1.0 / 1024, scalar2=None, op0=ALU.mult),
                         reads=['ss'], writes=['ss'])
                    S.op('dve', lambda e: e.tensor_scalar(out=vv, in0=vv, scalar1=ss[:, 0:1], scalar2=None, op0=ALU.add),
                         reads=['pz', 'ss'], writes=['pz'])
                    S.op('dve', lambda e: e.memset(ss[:, 1:2], 0.0), writes=['ss1'])
                    S.op('act', lambda e: e.activation(out=T[:], in_=vv, func=AF.Square, accum_out=ss[:, 1:2]),
                         reads=['pz', 'ss1'], writes=['T', 'ss1'])
                    rstd_from_ss(ss[:, 1:2], 1024, LN_EPS, 'ss1')
                    S.op('dve', lambda e: e.scalar_tensor_tensor(out=vv, in0=vv, scalar=ss[:, 1:2], in1=cst['lng'][:],
                                                                 op0=ALU.mult, op1=ALU.mult), reads=['pz', 'ss1', 'c_lng'], writes=['pz'])
                    S.op('dve', lambda e: e.tensor_tensor(out=vv, in0=vv, in1=cst['lnb'][:], op=ALU.add), reads=['pz', 'c_lnb'], writes=['pz'])
                    if tt == 15:
                        S.dma('sp', chunkv_p[:, :], vv, reads=['pz'], writes=['o_chunkv_p'])
                    if samp:
                        S.dma('sp', chunkv_s[:, :], pz[0:NS, 1024:2048], reads=['pz'], writes=['o_chunkv_s'])
                        S.op('dve', lambda e: e.memset(pz[64:128, 1024:2048], 0.0), reads=['pz'], writes=['pz'])
                    S.op('dve', lambda e: e.tensor_copy(out=vb[:], in_=vv), reads=['pz'], writes=['vb'])
                    wmat = wsbs if samp else wsb
                    wkey = 'wsbs' if samp else 'wsb'
                    for h in range(8):
                        bank = h // 4
                        S.op('pe', lambda e, h=h, bank=bank: e.matmul(PS[:, bank, (h % 4) * 128:(h % 4 + 1) * 128], lhsT=wmat[:, h, :],
                                                                     rhs=vb[:, h * 128:(h + 1) * 128], start=True, stop=True),
                             reads=[wkey, 'vb'], writes=['ps%d' % bank])
                    bt = bsts if samp else bst
                    for h in range(8):
                        bank = h // 4
                        S.op('dve', lambda e, h=h, bank=bank: e.scalar_tensor_tensor(
                            out=ya[:, h * 128:(h + 1) * 128], in0=PS[:, bank, (h % 4) * 128:(h % 4 + 1) * 128], scalar=bt[:, h:h + 1],
                            in1=pz[:, h * 128:(h + 1) * 128], op0=ALU.add, op1=ALU.mult),
                             reads=['ps%d' % bank, 'pz', 'bst', 'bsts'], writes=['ya'])
                    S.dma('sp', yaP[orow:orow + nr, :], ya[0:nr, :], reads=['ya'], writes=['yaP%d' % tt])
                if samp:
                    S.dma('sp', q[0:NS, :], projS[16:80, 2048:WIN], reads=pk, writes=['q'])
                    S.dma('sp', pp[0:NS, :], projS[0:64, 2048:WIN], reads=pk, writes=['pp'])
                else:
                    S.dma('sp', q[:], projP[1 + tt * 128:1 + (tt + 1) * 128, 2048:WIN], reads=pk, writes=['q'])
                    S.dma('sp', pp[:], projP[tt * 128:(tt + 1) * 128, 2048:WIN], reads=pk, writes=['pp'])
                S.op('dve', lambda e: e.tensor_tensor(out=pp[:], in0=pp[:], in1=q[:], op=ALU.subtract), reads=['pp', 'q'], writes=['pp'])
                S.op('pool', lambda e: e.tensor_tensor(out=pp[:], in0=pp[:], in1=cst['mu'][:], op=ALU.mult), reads=['pp', 'c_mu'], writes=['pp'])
                S.op('dve', lambda e: e.tensor_tensor(out=q[:], in0=q[:], in1=pp[:], op=ALU.add), reads=['pp', 'q'], writes=['q'])
                r_ = q[:, 0:1024]; k_ = q[:, 1024:2048]; v_ = q[:, 2048:3072]
                S.op('act', lambda e: e.activation(out=q[:, 3072:3136], in_=q[:, 3072:3136], func=AF.Tanh), reads=['q'], writes=['q'])
                S.op('act', lambda e: e.activation(out=q[:, 3200:3360], in_=q[:, 3200:3360], func=AF.Sigmoid), reads=['q'], writes=['q'])
                for (c0, cw, dst, dk) in ((3072, 128, lT1, 'lT1'), (3200, 128, lT2, 'lT2'), (3328, 32, lT3, 'lT3')):
                    S.op('pe', lambda e, c0=c0, cw=cw: e.transpose(out=PS[0:cw, 0, 0:128], in_=q[:, c0:c0 + cw], identity=ident[:]),
                         reads=['q', 'ident'], writes=['ps0'])
                    copy('dve', dst[0:cw, :], PS[0:cw, 0, 0:128], ['ps0'], [dk])
                for j in range(2):
                    cs = slice(j * 512, (j + 1) * 512)
                    S.op('pe', lambda e, j=j, cs=cs: e.matmul(PS[:, 2 + j, :], lhsT=lT1[0:64, :], rhs=waup[0:64, cs], start=True, stop=True),
                         reads=['lT1', 'waup'], writes=['ps%d' % (2 + j)])
                    S.op('pe', lambda e, j=j, cs=cs: e.matmul(PS[:, 4 + j, :], lhsT=lT1[64:128, :], rhs=waup[64:128, cs], start=True, stop=True),
                         reads=['lT1', 'waup'], writes=['ps%d' % (4 + j)])
                    S.op('pe', lambda e, j=j, cs=cs: e.matmul(PS[:, 6 + j, :], lhsT=lT2[:, :], rhs=gup1[:, cs], start=True, stop=False),
                         reads=['lT2', 'gup1'], writes=['ps%d' % (6 + j)])
                    S.op('pe', lambda e, j=j, cs=cs: e.matmul(PS[:, 6 + j, :], lhsT=lT3[0:32, :], rhs=gup2[0:32, cs], start=False, stop=True),
                         reads=['lT3', 'gup2'], writes=['ps%d' % (6 + j)])
                ps2 = lambda b: PS[:, b:b + 2, :].rearrange("p a b -> p (a b)")
                for j in range(2):
                    cs = slice(j * 512, (j + 1) * 512)
                    S.op('dve', lambda e, j=j, cs=cs: e.tensor_tensor(out=W[:, cs], in0=PS[:, 2 + j, :], in1=cst['w0'][:, cs], op=ALU.add),
                         reads=['ps%d' % (2 + j), 'c_w0'], writes=['W'])
                    S.op('dve', lambda e, j=j, cs=cs: e.tensor_tensor(out=A[:, cs], in0=PS[:, 4 + j, :], in1=cst['a0'][:, cs], op=ALU.add),
                         reads=['ps%d' % (4 + j), 'c_a0'], writes=['A'])
                    copy('act', G[:, cs], PS[:, 6 + j, :], ['ps%d' % (6 + j)], ['G'])
                S.op('act', lambda e: e.activation(out=W[:], in_=W[:], func=AF.Sigmoid), reads=['W'], writes=['W'])
                if samp:
                    S.op('act', lambda e: e.activation(out=W[:], in_=W[:], func=AF.Exp, scale=-math.exp(-0.5), bias=eps_tiles[0.0][:, 0:1]), reads=['W', 'epsc'], writes=['W'])
                else:
                    S.op('act', lambda e: e.activation(out=W[:], in_=W[:], func=AF.Copy, scale=-math.exp(-0.5)), reads=['W'], writes=['W'])
                S.op('act', lambda e: e.activation(out=A[:], in_=A[:], func=AF.Sigmoid), reads=['A'], writes=['A'])
                S.op('dve', lambda e: e.tensor_tensor(out=KK[:], in0=k_, in1=cst['kk'][:], op=ALU.mult), reads=['q', 'c_kk'], writes=['KK'])
                S.op('pool', lambda e: e.tensor_tensor(out=T[:], in0=KK[:], in1=KK[:], op=ALU.mult), reads=['KK'], writes=['T'])
                S.op('dve', lambda e: e.tensor_reduce(out=st16[:], in_=v3(T), axis=AX.X, op=ALU.add), reads=['T'], writes=['st16'])
                S.op('dve', lambda e: e.tensor_scalar(out=st16[:], in0=st16[:], scalar1=1e-24, scalar2=None, op0=ALU.max), reads=['st16'], writes=['st16'])
                rstd_from_ss(st16[:], 1.0, 0.0, 'st16')
                S.op('dve', lambda e: e.tensor_tensor(out=v3(KK), in0=v3(KK), in1=bc16(st16), op=ALU.mult), reads=['KK', 'st16'], writes=['KK'])
                S.op('dve', lambda e: e.scalar_tensor_tensor(out=T[:], in0=A[:], scalar=-1.0, in1=cst['ka'][:], op0=ALU.add, op1=ALU.mult),
                     reads=['A', 'c_ka'], writes=['T'])
                S.op('dve', lambda e: e.scalar_tensor_tensor(out=K2[:], in0=T[:], scalar=1.0, in1=k_, op0=ALU.add, op1=ALU.mult),
                     reads=['T', 'q'], writes=['K2'])
                S.op('pool', lambda e: e.tensor_tensor(out=BS[:], in0=KK[:], in1=A[:], op=ALU.mult), reads=['KK', 'A'], writes=['BS'])
                S.op('act', lambda e: e.activation(out=AS[:], in_=KK[:], func=AF.Copy, scale=-1.0), reads=['KK'], writes=['AS'])
                if own:
                    S.op('dve', lambda e: e.tensor_tensor(out=T[:], in0=r_, in1=K2[:], op=ALU.mult), reads=['q', 'K2'], writes=['T'])
                    S.op('dve', lambda e: e.tensor_tensor(out=T[:], in0=T[:], in1=cst['rk'][:], op=ALU.mult), reads=['T', 'c_rk'], writes=['T'])
                    S.op('dve', lambda e: e.tensor_reduce(out=st16b[:], in_=v3(T), axis=AX.X, op=ALU.add), reads=['T'], writes=['st16b'])
                    S.op('dve', lambda e: e.tensor_tensor(out=v3(T), in0=q[:, 2048:3072].rearrange("p (h j) -> p h j", h=16), in1=bc16(st16b), op=ALU.mult),
                         reads=['q', 'st16b'], writes=['T'])
                    S.dma('sp', bonP[orow:orow + nr, :], T[0:nr, :], reads=['T'], writes=['bonP%d' % tt])
                    S.dma('sp', gP[orow:orow + nr, :], G[0:nr, :], reads=['G'], writes=['gP%d' % tt])
                dst = scS if samp else scP
                r0 = 0 if samp else tt * 128
                for nm, src, key in (("r", r_, 'q'), ("w", W[:], 'W'), ("k", K2[:], 'K2'), ("a", AS[:], 'AS'), ("b", BS[:], 'BS'), ("v", v_, 'q')):
                    S.dma('sp', dst[nm][r0:r0 + nr, :], src[0:nr, :], reads=[key], writes=['sc_%s_%d' % (nm, tt)])

        S.barrier()

        def scan(p3, name, G_, I_, J_, nsteps, TB, load_j, load_v, store_y, state_init, state_out):
            GI = G_ * I_
            St = sbuf(p3, name + "_S", [128, G_, I_, J_])
            tmp = sbuf(p3, name + "_tmp", [128, G_, I_, J_])
            tmp2 = [sbuf(p3, name + "_tmp2_%d" % i, [128, G_, I_, J_]) for i in range(2)]
            sa = sbuf(p3, name + "_sa", [128, G_, I_])
            jv = [{qn: sbuf(p3, "%s_j%s%d" % (name, qn, i), [128, TB, G_, J_]) for qn in QN} for i in range(2)]
            iv = [sbuf(p3, "%s_iv%d" % (name, i), [128, TB, G_, I_]) for i in range(2)]
            yv = [sbuf(p3, "%s_yv%d" % (name, i), [128, TB, G_, I_]) for i in range(2)]
            sk = name + '_S'
            state_init(St, sk)
            nblk = nsteps // TB
            bcI = lambda ap: ap.unsqueeze(2).to_broadcast([128, G_, I_, J_])
            bcJ = lambda ap: ap.unsqueeze(3).to_broadcast([128, G_, I_, J_])

            def load_blk(b):
                i = b % 2
                for qn in QN:
                    load_j(qn, b, jv[i][qn], '%s_j%s%d' % (name, qn, i))
                load_v(b, iv[i], '%s_iv%d' % (name, i))
            load_blk(0)
            for b in range(nblk):
                i = b % 2
                if b + 1 < nblk:
                    load_blk(b + 1)
                jk = {qn: '%s_j%s%d' % (name, qn, i) for qn in QN}
                ivk = '%s_iv%d' % (name, i)
                yk = '%s_yv%d' % (name, i)
                for t in range(TB):
                    a_t = jv[i]['a'][:, t]; w_t = jv[i]['w'][:, t]; b_t = jv[i]['b'][:, t]; k_t = jv[i]['k'][:, t]; r_t = jv[i]['r'][:, t]
                    v_t = iv[i][:, t]
                    t2 = tmp2[t % 2]; t2k = '%s_t2_%d' % (name, t % 2)
                    S.op('pool', lambda e, t2=t2, v_t=v_t, k_t=k_t: e.tensor_tensor(out=t2[:], in0=bcJ(v_t), in1=bcI(k_t), op=ALU.mult),
                         reads=[ivk, jk['k']], writes=[t2k])
                    S.op('dve', lambda e, a_t=a_t: e.tensor_tensor(out=tmp[:], in0=St[:], in1=bcI(a_t), op=ALU.mult),
                         reads=[sk, jk['a']], writes=[name + '_tmp'])
                    S.op('dve', lambda e: e.tensor_reduce(out=sa[:], in_=tmp[:], axis=AX.X, op=ALU.add), reads=[name + '_tmp'], writes=[name + '_sa'])
                    S.op('dve', lambda e, w_t=w_t: e.tensor_tensor(out=St[:], in0=St[:], in1=bcI(w_t), op=ALU.mult),
                         reads=[sk, jk['w']], writes=[sk])
                    S.op('dve', lambda e, b_t=b_t: e.tensor_tensor(out=tmp[:], in0=bcJ(sa[:]), in1=bcI(b_t), op=ALU.mult),
                         reads=[name + '_sa', jk['b']], writes=[name + '_tmp'])
                    S.op('dve', lambda e: e.tensor_tensor(out=St[:], in0=St[:], in1=tmp[:], op=ALU.add), reads=[sk, name + '_tmp'], writes=[sk])
                    S.op('dve', lambda e, t2=t2: e.tensor_tensor(out=St[:], in0=St[:], in1=t2[:], op=ALU.add), reads=[sk, t2k], writes=[sk])
                    S.op('dve', lambda e, r_t=r_t: e.tensor_tensor(out=tmp[:], in0=St[:], in1=bcI(r_t), op=ALU.mult),
                         reads=[sk, jk['r']], writes=[name + '_tmp'])
                    S.op('dve', lambda e, t=t: e.tensor_reduce(out=yv[i][:, t], in_=tmp[:], axis=AX.X, op=ALU.add),
                         reads=[name + '_tmp'], writes=[yk])
                store_y(b, yv[i], yk)
            state_out(St, sk)

        sc_keys_s = lambda nm: ['sc_%s_16' % nm]
        with ExitStack() as p3:
            def s_load_j(qn, b, tile_, key):
                src = bass.AP(tensor=scS[qn].tensor, offset=scS[qn].offset, ap=[[128, 128], [16384, 4], [1, 128]])
                S.dma('sp', tile_[:].rearrange("p t g j -> p t (g j)"), src, reads=sc_keys_s(qn), writes=[key])

            def s_load_v(b, tile_, key):
                src = bass.AP(tensor=scS['v'].tensor, offset=scS['v'].offset, ap=[[128, 128], [16384, 4], [1, 128]])
                S.dma('sp', tile_[:].rearrange("p t g i -> p t (g i)"), src, reads=sc_keys_s('v'), writes=[key])

            def s_store_y(b, tile_, key):
                dst = bass.AP(tensor=yscS.tensor, offset=yscS.offset, ap=[[128, 128], [16384, 4], [1, 128]])
                S.dma('sp', dst, tile_[:].rearrange("p t g i -> p t (g i)"), reads=[key], writes=['yscS'])

            def s_init(St, sk):
                S.dma('sp', St[:].rearrange("p g i j -> p (g i j)"), wkv0[:, :], writes=[sk])

            def s_out(St, sk):
                S.dma('sp', wkv_s[:, :], St[:].rearrange("p g i j -> p (g i j)"), reads=[sk], writes=['o_wkv_s'])
            scan(p3, "ss", 2, 64, 64, 4, 4, s_load_j, s_load_v, s_store_y, s_init, s_out)
        S.barrier()
        with ExitStack() as p3:
            m_su = sbuf(p3, "m_su", [128, 128]); m_sl = sbuf(p3, "m_sl", [128, 128])
            S.op('dve', lambda e: e.tensor_scalar(out=m_su[:], in0=io_fp[:], scalar1=0.0, scalar2=None, op0=ALU.is_gt), reads=['io_fp'], writes=['m_su'])
            S.op('dve', lambda e: e.tensor_scalar(out=m_sl[:], in0=io_fp[:], scalar1=0.0, scalar2=None, op0=ALU.is_lt), reads=['io_fp'], writes=['m_sl'])
            CQ = ["r", "w", "k", "v", "a", "b"]
            IN = [{qn: sbuf(p3, "cin_%s%d" % (qn, i), [128, 1024]) for qn in CQ} for i in range(2)]
            E1 = sbuf(p3, "cE1", [128, 1024]); E2 = sbuf(p3, "cE2", [128, 1024]); E3 = sbuf(p3, "cE3", [128, 1024])
            TT = {qn: sbuf(p3, "cTT_" + qn, [128, 8, 128], BF16) for qn in ("r", "a", "b", "k")}
            TT["e"] = sbuf(p3, "cTT_e", [128, 8, 128])
            TQb = {qn: sbuf(p3, "cTQb_" + qn, [128, 1024], BF16) for qn in ("r", "a", "b", "k")}
            STb = sbuf(p3, "cSTb", [128, 8, 64], BF16)
            MK = {kn: sbuf(p3, "cM_" + kn, [128, 16, 128]) for kn in ("AbT", "ArbT", "AkT", "ArkT", "Tt")}
            for kn in ("Ab", "AbTb", "Pb", "Qb", "Ttb"):
                MK[kn] = sbuf(p3, "cM_" + kn, [128, 16, 128], BF16)
            Xt = sbuf(p3, "cX", [128, 1024]); Ut = sbuf(p3, "cU", [128, 1024]); Yt = sbuf(p3, "cY", [128, 1024])
            ST = sbuf(p3, "cST", [128, 8, 64])
            S.op('dve', lambda e: e.memset(ST[:], 0.0), writes=['cST'])
            S.op('dve', lambda e: e.memset(STb[:], 0.0), writes=['cSTb'])
            bk = {'n': 0}

            def nbank():
                bk['n'] += 1
                return bk['n'] % 8

            def load_tile(tt):
                i = tt % 2
                for qn in CQ:
                    S.dma('sp', IN[i][qn][:], scP[qn][tt * 128:(tt + 1) * 128, :], reads=['sc_%s_%d' % (qn, tt)], writes=['cin_%s%d' % (qn, i)])
            bc4 = lambda m: m[:].unsqueeze(1).to_broadcast([128, 4, 128])
            CH_STOP = int(os.environ.get('CH_STOP', '9'))
            load_tile(0)
            for tt in range(16):
                i = tt % 2
                if tt + 1 < 16:
                    load_tile(tt + 1)
                X_ = IN[i]
                kx = {qn: 'cin_%s%d' % (qn, i) for qn in CQ}
                if CH_STOP >= 1:
                    for j in range(2):
                        S.op('pe', lambda e, j=j: e.matmul(PS[:, j, :], lhsT=cmask[:], rhs=X_['w'][:, j * 512:(j + 1) * 512], start=True, stop=True),
                             reads=['cmask', kx['w']], writes=['ps%d' % j])
                    for j in range(2):
                        cs = slice(j * 512, (j + 1) * 512)
                        S.op('dve', lambda e, j=j, cs=cs: e.tensor_copy(out=Xt[:, cs], in_=PS[:, j, :]), reads=['ps%d' % j], writes=['cX'])
                    zb_ = eps_tiles[0.0][:, 0:1]
                    S.op('act', lambda e: e.activation(out=E1[:], in_=Xt[:], func=AF.Exp, scale=1.0, bias=zb_), reads=['cX', 'epsc'], writes=['cE1'])
                    S.op('act', lambda e: e.activation(out=E2[:], in_=Xt[:], func=AF.Exp, scale=-1.0, bias=zb_), reads=['cX', 'epsc'], writes=['cE2'])
                    S.op('dve', lambda e: e.tensor_tensor(out=E3[:], in0=Xt[:], in1=X_['w'][:], op=ALU.subtract), reads=['cX', kx['w']], writes=['cE3'])
                    S.op('act', lambda e: e.activation(out=E3[:], in_=E3[:], func=AF.Exp, scale=1.0, bias=eps_tiles[0.0][:, 0:1]), reads=['cE3', 'epsc'], writes=['cE3'])
                    S.op('dve', lambda e: e.tensor_tensor(out=TQb['r'][:], in0=X_['r'][:], in1=E1[:], op=ALU.mult), reads=[kx['r'], 'cE1'], writes=['cTQb_r'])
                    S.op('pool', lambda e: e.tensor_tensor(out=TQb['a'][:], in0=X_['a'][:], in1=E3[:], op=ALU.mult), reads=[kx['a'], 'cE3'], writes=['cTQb_a'])
                    S.op('dve', lambda e: e.tensor_tensor(out=X_['b'][:], in0=X_['b'][:], in1=E2[:], op=ALU.mult), reads=[kx['b'], 'cE2'], writes=[kx['b']])
                    S.op('pool', lambda e: e.tensor_tensor(out=X_['k'][:], in0=X_['k'][:], in1=E2[:], op=ALU.mult), reads=[kx['k'], 'cE2'], writes=[kx['k']])
                    S.op('act', lambda e: e.activation(out=TQb['b'][:], in_=X_['b'][:], func=AF.Copy), reads=[kx['b']], writes=['cTQb_b'])
                    S.op('act', lambda e: e.activation(out=TQb['k'][:], in_=X_['k'][:], func=AF.Copy), reads=[kx['k']], writes=['cTQb_k'])
                if CH_STOP >= 2:
                    for qn in ("r", "a", "b", "k"):
                        bank = nbank()
                        psb_ = PS[:, bank, :].bitcast(BF16)
                        for c in range(8):
                            S.op('pe', lambda e, c=c, qn=qn, psb_=psb_: e.transpose(out=psb_[:, c * 128:(c + 1) * 128], in_=TQb[qn][:, c * 128:(c + 1) * 128], identity=identb[:]),
                                 reads=['cTQb_' + qn, 'identb'], writes=['ps%d' % bank])
                        copy(evac_eng(), TT[qn][:], psb_.rearrange("p (a b) -> p a b", a=8), ['ps%d' % bank], ['cTT_' + qn])
                    transpose_to(lambda c0, nb: TT["e"][:, c0:c0 + nb, :], E1, 8, 'cE1', 'cTT_e', psb=(2, 3, 4, 5))
                hT = lambda qn, h: TT[qn][64 * (h % 2):64 * (h % 2) + 64, h // 2, :]
                if CH_STOP >= 3:
                    for g in range(4):
                        for kn, lq, rq, msk, mkey in (("AbT", "b", "a", m_su, 'm_su'), ("ArbT", "b", "r", cmask, 'cmask'), ("AkT", "k", "a", m_su, 'm_su'),
                                                      ("ArkT", "k", "r", cmask, 'cmask'), ("Ab", "a", "b", m_sl, 'm_sl')):
                            bank = nbank()
                            for hi in range(4):
                                h = 4 * g + hi
                                S.op('pe', lambda e, h=h, hi=hi, bank=bank, lq=lq, rq=rq: e.matmul(PS[:, bank, hi * 128:(hi + 1) * 128], lhsT=hT(lq, h), rhs=hT(rq, h),
                                                                                              start=True, stop=True),
                                     reads=['cTT_' + lq, 'cTT_' + rq], writes=['ps%d' % bank])
                            S.op('dve', lambda e, kn=kn, g=g, bank=bank, msk=msk: e.tensor_tensor(
                                out=MK[kn][:, 4 * g:4 * g + 4, :], in0=PS[:, bank, :].rearrange("p (a b) -> p a b", a=4), in1=bc4(msk), op=ALU.mult),
                                 reads=['ps%d' % bank, mkey], writes=['cM_%s_%d' % (kn, g)])
                if CH_STOP >= 4:
                    for g in range(4):
                        S.op('pool', lambda e, g=g: e.tensor_tensor(out=MK["Tt"][:, 4 * g:4 * g + 4, :], in0=MK["AbT"][:, 4 * g:4 * g + 4, :], in1=bc4(ident), op=ALU.add),
                             reads=['cM_AbT_%d' % g, 'ident'], writes=['cM_Tt_%d' % g])
                        S.op('pool', lambda e, g=g: e.tensor_copy(out=MK["AbTb"][:, 4 * g:4 * g + 4, :], in_=MK["AbT"][:, 4 * g:4 * g + 4, :]),
                             reads=['cM_AbT_%d' % g], writes=['cM_AbTb_%d' % g])
                        S.op('pool', lambda e, g=g: e.tensor_copy(out=MK["Ttb"][:, 4 * g:4 * g + 4, :], in_=MK["Tt"][:, 4 * g:4 * g + 4, :]),
                             reads=['cM_Tt_%d' % g], writes=['cM_Ttb_%d' % g])
                    Pn, Qn, Po, Qo = "Ab", "AbTb", "Pb", "Qb"
                    for lev in range(1, 7):
                        for g in range(4):
                            hs = slice(4 * g, 4 * g + 4)
                            bank = nbank()
                            for hi in range(4):
                                h = 4 * g + hi
                                S.op('pe', lambda e, h=h, hi=hi, bank=bank, Pn=Pn, Qn=Qn: e.matmul(PS[:, bank, hi * 128:(hi + 1) * 128], lhsT=MK[Qn][:, h, :], rhs=MK[Pn][:, h, :],
                                                                                              start=True, stop=True),
                                     reads=['cM_%s_%d' % (Pn, g), 'cM_%s_%d' % (Qn, g)], writes=['ps%d' % bank])
                            copy('act', MK[Po][:, hs, :], PS[:, bank, :].rearrange("p (a b) -> p a b", a=4), ['ps%d' % bank], ['cM_%s_%d' % (Po, g)])
                            if lev < 6:
                                bank = nbank()
                                for hi in range(4):
                                    h = 4 * g + hi
                                    S.op('pe', lambda e, h=h, hi=hi, bank=bank, Pn=Pn, Qn=Qn: e.matmul(PS[:, bank, hi * 128:(hi + 1) * 128], lhsT=MK[Pn][:, h, :], rhs=MK[Qn][:, h, :],
                                                                                                  start=True, stop=True),
                                         reads=['cM_%s_%d' % (Pn, g), 'cM_%s_%d' % (Qn, g)], writes=['ps%d' % bank])
                                copy('act', MK[Qo][:, hs, :], PS[:, bank, :].rearrange("p (a b) -> p a b", a=4), ['ps%d' % bank], ['cM_%s_%d' % (Qo, g)])
                            bank = nbank()
                            for hi in range(4):
                                h = 4 * g + hi
                                S.op('pe', lambda e, h=h, hi=hi, bank=bank, Po=Po: e.matmul(PS[:, bank, hi * 128:(hi + 1) * 128], lhsT=MK[Po][:, h, :], rhs=MK["Ttb"][:, h, :],
                                                                                       start=True, stop=True),
                                     reads=['cM_%s_%d' % (Po, g), 'cM_Ttb_%d' % g], writes=['ps%d' % bank])
                            S.op('dve', lambda e, hs=hs, bank=bank: e.tensor_tensor(out=MK["Tt"][:, hs, :], in0=MK["Tt"][:, hs, :],
                                                                                in1=PS[:, bank, :].rearrange("p (a b) -> p a b", a=4), op=ALU.add),
                                 reads=['ps%d' % bank, 'cM_Tt_%d' % g], writes=['cM_Tt_%d' % g])
                            if lev < 6:
                                S.op('pool', lambda e, hs=hs: e.tensor_copy(out=MK["Ttb"][:, hs, :], in_=MK["Tt"][:, hs, :]),
                                     reads=['cM_Tt_%d' % g], writes=['cM_Ttb_%d' % g])
                        Pn, Qn, Po, Qo = Po, Qo, Pn, Qn
                sth = lambda h: STb[64 * (h % 2):64 * (h % 2) + 64, h // 2, :]
                allg = lambda kn: ['cM_%s_%d' % (kn, g) for g in range(4)]
                if CH_STOP >= 5:
                    a0_ = nbank(); a1_ = nbank(); b0 = nbank(); b1 = nbank()
                    for h in range(16):
                        cs = slice((h % 8) * 64, (h % 8) * 64 + 64)
                        ba = a0_ if h < 8 else a1_
                        bb = b0 if h < 8 else b1
                        S.op('pe', lambda e, h=h, ba=ba, cs=cs: e.matmul(PS[:, ba, cs], lhsT=hT("a", h), rhs=sth(h), start=True, stop=True),
                             reads=['cTT_a', 'cSTb'], writes=['ps%d' % ba])
                        S.op('pe', lambda e, h=h, bb=bb, cs=cs: e.matmul(PS[:, bb, cs], lhsT=MK["AkT"][:, h, :], rhs=X_['v'][:, h * 64:(h + 1) * 64], start=True, stop=True),
                             reads=allg("AkT") + [kx['v']], writes=['ps%d' % bb])
                    copy('act', Xt[:, 0:512], PS[:, a0_, :], ['ps%d' % a0_], ['cX'])
                    copy('act', Xt[:, 512:1024], PS[:, a1_, :], ['ps%d' % a1_], ['cX'])
                    S.op('dve', lambda e: e.tensor_tensor(out=Xt[:, 0:512], in0=Xt[:, 0:512], in1=PS[:, b0, :], op=ALU.add), reads=['cX', 'ps%d' % b0], writes=['cX'])
                    S.op('dve', lambda e: e.tensor_tensor(out=Xt[:, 512:1024], in0=Xt[:, 512:1024], in1=PS[:, b1, :], op=ALU.add), reads=['cX', 'ps%d' % b1], writes=['cX'])
                if CH_STOP >= 6:
                    b0 = nbank(); b1 = nbank()
                    for h in range(16):
                        bank = b0 if h < 8 else b1
                        cs = slice((h % 8) * 64, (h % 8) * 64 + 64)
                        S.op('pe', lambda e, h=h, bank=bank, cs=cs: e.matmul(PS[:, bank, cs], lhsT=MK["Tt"][:, h, :], rhs=Xt[:, h * 64:(h + 1) * 64], start=True, stop=True),
                             reads=allg("Tt") + ['cX'], writes=['ps%d' % bank])
                    copy('act', Ut[:, 0:512], PS[:, b0, :], ['ps%d' % b0], ['cU'])
                    copy('dve', Ut[:, 512:1024], PS[:, b1, :], ['ps%d' % b1], ['cU'])
                if CH_STOP >= 7:
                    a0_ = nbank(); a1_ = nbank(); b0 = nbank(); b1 = nbank()
                    for h in range(16):
                        cs = slice((h % 8) * 64, (h % 8) * 64 + 64)
                        ba = a0_ if h < 8 else a1_
                        bb = b0 if h < 8 else b1
                        S.op('pe', lambda e, h=h, ba=ba, cs=cs: e.matmul(PS[:, ba, cs], lhsT=hT("r", h), rhs=sth(h), start=True, stop=True),
                             reads=['cTT_r', 'cSTb'], writes=['ps%d' % ba])
                        S.op('pe', lambda e, h=h, bb=bb, cs=cs: e.matmul(PS[:, bb, cs], lhsT=MK["ArbT"][:, h, :], rhs=Ut[:, h * 64:(h + 1) * 64], start=True, stop=False),
                             reads=allg("ArbT") + ['cU'], writes=['ps%d' % bb])
                        S.op('pe', lambda e, h=h, bb=bb, cs=cs: e.matmul(PS[:, bb, cs], lhsT=MK["ArkT"][:, h, :], rhs=X_['v'][:, h * 64:(h + 1) * 64], start=False, stop=True),
                             reads=allg("ArkT") + [kx['v']], writes=['ps%d' % bb])
                    if tt >= 8:
                        copy('act', Yt[:, 0:512], PS[:, a0_, :], ['ps%d' % a0_], ['cY'])
                        copy('act', Yt[:, 512:1024], PS[:, a1_, :], ['ps%d' % a1_], ['cY'])
                        S.op('dve', lambda e: e.tensor_tensor(out=Yt[:, 0:512], in0=Yt[:, 0:512], in1=PS[:, b0, :], op=ALU.add), reads=['cY', 'ps%d' % b0], writes=['cY'])
                        S.op('dve', lambda e: e.tensor_tensor(out=Yt[:, 512:1024], in0=Yt[:, 512:1024], in1=PS[:, b1, :], op=ALU.add), reads=['cY', 'ps%d' % b1], writes=['cY'])
                        S.dma('sp', yscP[tt * 128:(tt + 1) * 128, :], Yt[:], reads=['cY'], writes=['yscP%d' % tt])
                if CH_STOP < 7 and tt >= 8:
                    for j in range(2):
                        S.dma('sp', yscP[tt * 128:(tt + 1) * 128, j * 512:(j + 1) * 512], zero_t[:, :], reads=['zero_t'], writes=['yscP%d' % tt])
                if CH_STOP >= 8:
                    sb0 = nbank()
                    sb1 = (sb0 + 1) % 8
                    bk['n'] += 1
                    for hp in range(8):
                        bank = sb0 if hp < 4 else sb1
                        cs = slice((hp % 4) * 128, (hp % 4) * 128 + 128)
                        ps_ = slice(hp * 128, (hp + 1) * 128)
                        S.op('pe', lambda e, bank=bank, cs=cs, ps_=ps_: e.matmul(PS[:, bank, cs], lhsT=X_['b'][:, ps_], rhs=Ut[:, ps_], start=True, stop=False),
                             reads=[kx['b'], 'cU'], writes=['ps%d' % bank])
                        S.op('pe', lambda e, bank=bank, cs=cs, ps_=ps_: e.matmul(PS[:, bank, cs], lhsT=X_['k'][:, ps_], rhs=X_['v'][:, ps_], start=False, stop=True),
                             reads=[kx['k'], kx['v']], writes=['ps%d' % bank])
                    for h2 in range(2):
                        pr = slice(64 * h2, 64 * h2 + 64)
                        for bank, hp0 in ((sb0, 0), (sb1, 4)):
                            psv = PS[pr, bank, :].rearrange("p (a x) -> p a x", a=4)[:, :, 64 * h2:64 * h2 + 64]
                            S.op('dve', lambda e, pr=pr, psv=psv, hp0=hp0: e.tensor_tensor(out=ST[pr, hp0:hp0 + 4, :], in0=ST[pr, hp0:hp0 + 4, :], in1=psv, op=ALU.add),
                                 reads=['cST', 'ps%d' % bank], writes=['cST'])
                        S.op('dve', lambda e, pr=pr: e.tensor_tensor(out=ST[pr, :, :], in0=ST[pr, :, :], in1=TT["e"][pr, :, 127:128].to_broadcast([64, 8, 64]), op=ALU.mult),
                             reads=['cST', 'cTT_e'], writes=['cST'])
                    S.op('pool', lambda e: e.tensor_copy(out=STb[:], in_=ST[:]), reads=['cST'], writes=['cSTb'])
            WO = sbuf(p3, "cWO", [64, 8, 128])
            for hp in range(8):
                bank = 0 if hp < 4 else 1
                S.op('pe', lambda e, hp=hp, bank=bank: e.transpose(out=PS[0:64, bank, (hp % 4) * 128:(hp % 4 + 1) * 128], in_=ST[:, hp, :], identity=ident[:]),
                     reads=['cST', 'ident'], writes=['ps%d' % bank])
            copy('dve', WO[:, 0:4, :], PS[0:64, 0, :].rearrange("p (a b) -> p a b", a=4), ['ps0'], ['cWO'])
            copy('dve', WO[:, 4:8, :], PS[0:64, 1, :].rearrange("p (a b) -> p a b", a=4), ['ps1'], ['cWO'])
            for h2 in range(2):
                dst = bass.AP(tensor=wkv_p.tensor, offset=wkv_p.offset + h2 * 4096, ap=[[64, 64], [8192, 8], [1, 64]])
                S.dma('sp', dst, WO[:, :, h2 * 64:(h2 + 1) * 64], reads=['cWO'], writes=['o_wkv_p%d' % h2])

        S.barrier()
        with ExitStack() as p4:
            g2 = sbuf(p4, "g2", [128, D]); gf = sbuf(p4, "gf", [128, D])
            gng = sbuf(p4, "gng", [128, 1024]); gnb = sbuf(p4, "gnb", [128, 1024])
            S.dma('sp', g2[:], bc_rows(norm2_g, D), writes=['g2'])
            S.dma('sp', gf[:], bc_rows(final_g, D), writes=['gf'])
            S.dma('sp', gng[:], bc_rows(gn_g, 1024), writes=['gng'])
            S.dma('sp', gnb[:], bc_rows(gn_b, 1024), writes=['gnb'])
            skt = sbuf(p4, "skt", [128, 2, 128])
            S.dma('sp', skt[:], skT[:, :, :], writes=['skt'])
            Hs = [sbuf(p4, "H%d" % i, [128, D]) for i in range(2)]
            B1 = sbuf(p4, "B1", [128, D])
            B2 = sbuf(p4, "B2", [128, D])
            B3 = sbuf(p4, "B3", [128, D])
            TB_ = sbuf(p4, "TB_", [128, 16, 128], BF16)
            QT = sbuf(p4, "QT", [128, 16, 128])
            xn2b = sbuf(p4, "xn2b", [128, D], BF16)
            wb = [sbuf(p4, "wb%d" % i, [128, 16, 512], BF16) for i in range(2)]
            NU = 6
            U = [sbuf(p4, "U%d" % i, [128, D]) for i in range(NU)]
            Vb = [sbuf(p4, "Vb%d" % i, [128, D], BF16) for i in range(2)]
            OH = sbuf(p4, "OH", [128, 4, 256])
            top = sbuf(p4, "top", [128, 16, 16]); tidx = sbuf(p4, "tidx", [128, 16, 16]); tiu = sbuf(p4, "tiu", [128, 16], U32)
            wk = sbuf(p4, "wk", [128, 256])
            bsv = sbuf(p4, "bsv", [128, 8, 16]); bidx = sbuf(p4, "bidx", [128, 8, 16]); eid = sbuf(p4, "eid", [128, 128])
            gat = sbuf(p4, "gat", [128, 128]); zz = sbuf(p4, "zz", [128, 8]); nmx = sbuf(p4, "nmx", [128, 8])
            eidTs = [sbuf(p4, "eidT%d" % i, [128, 128], I32) for i in range(2)]; gatTs = [sbuf(p4, "gatT%d" % i, [128, 128]) for i in range(2)]
            actTs = [sbuf(p4, "actT%d" % i, [128, 128]) for i in range(2)]
            Ln = [sbuf(p4, "Ln%d" % i, [128, 128], BF16) for i in range(4)]
            idc = [sbuf(p4, "idc%d" % i, [128, 1], I32) for i in range(4)]
            ss = sbuf(p4, "ss", [128, 4]); st16 = sbuf(p4, "st16", [128, 16]); st16b = sbuf(p4, "st16b", [128, 16])
            v3 = lambda ap: ap.rearrange("p (h j) -> p h j", h=16)
            bc16 = lambda t: t[:, 0:16].unsqueeze(2).to_broadcast([128, 16, 64])
            wload = {'n': 0}

            def big_mm(wsrc, dst_fn, post):
                for cbk in range(4):
                    i = wload['n'] % 2
                    wload['n'] += 1
                    w = wb[i]; wkey = 'wb%d' % i
                    S.dma('pool', w[:], wsrc.rearrange("(a p) c -> p a c", p=128)[:, :, cbk * 512:(cbk + 1) * 512], writes=[wkey])
                    bank = 2 + (cbk % 2)
                    for dc in range(16):
                        S.op('pe', lambda e, dc=dc, bank=bank, w=w: e.matmul(PS[:, bank, :], lhsT=TB_[:, dc, :], rhs=w[:, dc, :],
                                                                         start=(dc == 0), stop=(dc == 15)),
                             reads=['TB_', wkey], writes=['ps%d' % bank])
                    post(cbk, bank)

            def tile_front(ot):
                samp = ot == 8
                nr = NS if samp else 128
                orow = ot * 128
                tt = ot + 8
                H = Hs[ot % 2]; hk = 'H%d' % (ot % 2)
                eidT = eidTs[ot % 2]; ek = 'eidT%d' % (ot % 2)
                gatT = gatTs[ot % 2]; gk = 'gatT%d' % (ot % 2)
                actT = actTs[ot % 2]; ak = 'actT%d' % (ot % 2)
                cat = B1
                S.op('pool', lambda e: e.memset(B2[:], 0.0), writes=['B2'])
                S.op('pool', lambda e: e.memset(cat[:], 0.0), writes=['B1'])
                S.op('pool', lambda e: e.memset(H[:], 0.0), writes=[hk])
                yt = B2[:, 0:1024]; stg2 = B2[:, 1024:2048]
                if samp:
                    S.dma('sp', B2[0:NS, 0:1024], yscS[:, :], reads=['yscS'], writes=['B2'])
                    S.dma('sp', H[0:NS, :], xs[:, :], writes=[hk])
                else:
                    S.dma('sp', yt, yscP[NT - NOWN + orow:NT - NOWN + orow + 128, :], reads=['yscP%d' % tt], writes=['B2'])
                    S.dma('sp', H[:], xseq[NT - NOWN + orow:NT - NOWN + orow + 128, :], writes=[hk])
                S.dma('sp', cat[0:nr, 0:1024], yaP[orow:orow + nr, :], reads=['yaP%d' % tt], writes=['B1'])
                S.op('dve', lambda e: e.tensor_reduce(out=st16[:], in_=v3(yt), axis=AX.X, op=ALU.add), reads=['B2'], writes=['st16'])
                S.op('dve', lambda e: e.tensor_scalar(out=st16[:], in0=st16[:], scalar1=-1.0 / 64, scalar2=None, op0=ALU.mult), reads=['st16'], writes=['st16'])
                S.op('dve', lambda e: e.tensor_tensor(out=v3(yt), in0=v3(yt), in1=bc16(st16), op=ALU.add), reads=['B2', 'st16'], writes=['B2'])
                S.op('dve', lambda e: e.tensor_tensor(out=stg2, in0=yt, in1=yt, op=ALU.mult), reads=['B2'], writes=['B2s'])
                S.op('dve', lambda e: e.tensor_reduce(out=st16b[:], in_=v3(stg2), axis=AX.X, op=ALU.add), reads=['B2s'], writes=['st16b'])
                rstd_from_ss(st16b[:], 64.0, GN_EPS, 'st16b')
                S.op('dve', lambda e: e.tensor_tensor(out=v3(yt), in0=v3(yt), in1=bc16(st16b), op=ALU.mult), reads=['B2', 'st16b'], writes=['B2'])
                S.op('dve', lambda e: e.tensor_tensor(out=yt, in0=yt, in1=gng[:], op=ALU.mult), reads=['B2', 'gng'], writes=['B2'])
                S.op('dve', lambda e: e.tensor_tensor(out=yt, in0=yt, in1=gnb[:], op=ALU.add), reads=['B2', 'gnb'], writes=['B2'])
                S.dma('sp', B2[0:nr, 1024:2048], bonP[orow:orow + nr, :], reads=['bonP%d' % tt, 'B2s'], writes=['B2s'])
                S.op('dve', lambda e: e.tensor_tensor(out=yt, in0=yt, in1=stg2, op=ALU.add), reads=['B2', 'B2s'], writes=['B2'])
                S.dma('sp', B2[0:nr, 1024:2048], gP[orow:orow + nr, :], reads=['gP%d' % tt, 'B2', 'B2s'], writes=['B2s'])
                S.op('dve', lambda e: e.tensor_tensor(out=cat[:, 1024:2048], in0=yt, in1=stg2, op=ALU.mult), reads=['B2', 'B2s', 'B1'], writes=['B1'])
                transpose_to(lambda c0, nb: TB_[:, c0:c0 + nb, :], cat, 16, 'B1', 'TB_')
                big_mm(w_out, None, lambda cbk, bank: S.op('dve', lambda e: e.tensor_tensor(
                    out=H[:, cbk * 512:(cbk + 1) * 512], in0=PS[:, bank, :], in1=H[:, cbk * 512:(cbk + 1) * 512], op=ALU.add),
                    reads=['ps%d' % bank, hk], writes=[hk]))
                xn2 = B1
                rmsnorm(H, g2, hk, 'g2', ss, B2, outt=xn2, okey='B1', jkey='B2')
                S.op('act', lambda e: e.activation(out=xn2b[:], in_=xn2[:], func=AF.Copy), reads=['B1'], writes=['xn2b'])
                transpose_to(lambda c0, nb: TB_[:, c0:c0 + nb, :], xn2, 16, 'B1', 'TB_')
                qt = B2
                big_mm(w_q, None, lambda cbk, bank: copy(evac_eng(), qt[:, cbk * 512:(cbk + 1) * 512], PS[:, bank, :], ['ps%d' % bank], ['B2']))
                transpose_to(lambda c0, nb: QT[:, c0:c0 + nb, :], qt, 16, 'B2', 'QT')
                sc = B3
                for hc in range(16):
                    bank = hc // 4
                    S.op('pe', lambda e, hc=hc, bank=bank: e.matmul(PS[:, bank, (hc % 4) * 128:(hc % 4 + 1) * 128], lhsT=QT[:, hc, :],
                                                                   rhs=skt[:, hc % 2, :], start=True, stop=True),
                         reads=['QT', 'skt'], writes=['ps%d' % bank])
                for b4 in range(4):
                    copy(evac_eng(), sc[:, b4 * 512:(b4 + 1) * 512], PS[:, b4, :], ['ps%d' % b4], ['B3'])

                def top16(src_ap, n, vals, idxf, vkey, ikey, skey):
                    cur = src_ap
                    for half in range(2):
                        S.op('dve', lambda e, cur=cur, half=half: e.max(out=vals[:, half * 8:(half + 1) * 8], in_=cur), reads=[skey, 'wk'], writes=[vkey])
                        S.op('dve', lambda e, cur=cur, half=half: e.max_index(out=tiu[:, half * 8:(half + 1) * 8], in_max=vals[:, half * 8:(half + 1) * 8], in_values=cur),
                             reads=[skey, 'wk', vkey], writes=['tiu'])
                        if half == 0:
                            S.op('dve', lambda e, cur=cur: e.match_replace(out=wk[:, 0:n], in_to_replace=vals[:, 0:8], in_values=cur, imm_value=-1e30),
                                 reads=[skey, vkey], writes=['wk'])
                            cur = wk[:, 0:n]
                    S.op('dve', lambda e: e.tensor_copy(out=idxf, in_=tiu[:]), reads=['tiu'], writes=[ikey])

                for hc in range(16):
                    top16(sc[:, hc * 128:(hc + 1) * 128], 128, top[:, hc, :], tidx[:, hc, :], 'top', 'tidx', 'B3')
                cand = B2; cid = B3
                t4 = top[:].rearrange("p (h c) k -> p h c k", c=2)
                i4 = tidx[:].rearrange("p (h c) k -> p h c k", c=2)
                c4 = lambda t_: t_[:].rearrange("p (h a b) -> p h a b", h=8, a=16)
                S.op('dve', lambda e: e.tensor_tensor(out=c4(cand), in0=t4[:, :, 0, :].unsqueeze(3).to_broadcast([128, 8, 16, 16]),
                                                      in1=t4[:, :, 1, :].unsqueeze(2).to_broadcast([128, 8, 16, 16]), op=ALU.add),
                     reads=['top'], writes=['B2'])
                for h in range(8):
                    S.op('dve', lambda e, h=h: e.scalar_tensor_tensor(
                        out=cid[:, h * 256:(h + 1) * 256].rearrange("p (a b) -> p a b", a=16),
                        in0=tidx[:, 2 * h, :].unsqueeze(2).to_broadcast([128, 16, 16]), scalar=128.0,
                        in1=tidx[:, 2 * h + 1, :].unsqueeze(1).to_broadcast([128, 16, 16]), op0=ALU.mult, op1=ALU.add),
                         reads=['tidx'], writes=['B3'])
                for h in range(8):
                    top16(cand[:, h * 256:(h + 1) * 256], 256, bsv[:, h, :], bidx[:, h, :], 'bsv', 'bidx', 'B2')
                    for k0 in (0, 4, 8, 12):
                        S.op('dve', lambda e, h=h, k0=k0: e.tensor_tensor(out=OH[:], in0=io_f[:, 0:256].unsqueeze(1).to_broadcast([128, 4, 256]),
                                                                   in1=bidx[:, h, k0:k0 + 4].unsqueeze(2).to_broadcast([128, 4, 256]), op=ALU.is_equal),
                             reads=['io_f', 'bidx'], writes=['OH'])
                        S.op('dve', lambda e, h=h: e.tensor_tensor(out=OH[:], in0=OH[:], in1=cid[:, h * 256:(h + 1) * 256].unsqueeze(1).to_broadcast([128, 4, 256]),
                                                                   op=ALU.mult), reads=['OH', 'B3'], writes=['OH'])
                        S.op('dve', lambda e, h=h, k0=k0: e.tensor_reduce(out=eid[:, h * 16 + k0:h * 16 + k0 + 4], in_=OH[:], axis=AX.X, op=ALU.add), reads=['OH'], writes=['eid'])
                S.op('dve', lambda e: e.tensor_scalar(out=nmx[:], in0=bsv[:, :, 0], scalar1=-1.0, scalar2=None, op0=ALU.mult), reads=['bsv'], writes=['nmx'])
                S.op('dve', lambda e: e.memset(zz[:], 0.0), writes=['zz'])
                for h in range(8):
                    S.op('act', lambda e, h=h: e.activation(out=gat[:, h * 16:(h + 1) * 16], in_=bsv[:, h, :], func=AF.Exp, bias=nmx[:, h:h + 1], scale=1.0,
                                                            accum_out=zz[:, h:h + 1]), reads=['bsv', 'nmx', 'zz'], writes=['gat', 'zz'])
                S.op('dve', lambda e: e.reciprocal(out=zz[:], in_=zz[:]), reads=['zz'], writes=['zz'])
                S.op('dve', lambda e: e.tensor_tensor(out=gat[:].rearrange("p (h k) -> p h k", h=8), in0=gat[:].rearrange("p (h k) -> p h k", h=8),
                                                      in1=zz[:].unsqueeze(2).to_broadcast([128, 8, 16]), op=ALU.mult), reads=['gat', 'zz'], writes=['gat'])
                S.op('pe', lambda e: e.transpose(out=PS[:, 0, 0:128], in_=eid[:], identity=ident[:]), reads=['eid', 'ident'], writes=['ps0'])
                S.op('pe', lambda e: e.transpose(out=PS[:, 0, 128:256], in_=gat[:], identity=ident[:]), reads=['gat', 'ident'], writes=['ps0'])
                S.op('dve', lambda e: e.tensor_copy(out=eidT[:], in_=PS[:, 0, 0:128]), reads=['ps0'], writes=[ek])
                S.op('dve', lambda e: e.tensor_copy(out=gatT[:], in_=PS[:, 0, 128:256]), reads=['ps0'], writes=[gk])
                S.op('dve', lambda e: e.memset(actT[:], 0.0), writes=[ak])
            def tile_p1(ot):
                samp = ot == 8
                nr = NS if samp else 128
                orow = ot * 128
                tt = ot + 8
                H = Hs[ot % 2]; hk = 'H%d' % (ot % 2)
                eidT = eidTs[ot % 2]; ek = 'eidT%d' % (ot % 2)
                gatT = gatTs[ot % 2]; gk = 'gatT%d' % (ot % 2)
                actT = actTs[ot % 2]; ak = 'actT%d' % (ot % 2)
                junk = B2
                for n in range(nr):
                    i = n % NU
                    S.gather(U[i][:, :], u_tab[:, :], eidT[:, n:n + 1], NEXP, reads=[ek], writes=['U%d' % i])
                    sel = identb[:, n:n + 1].to_broadcast([128, 128])
                    pb = 4 * (n % 2)
                    for j in range(4):
                        S.op('pe', lambda e, j=j, sel=sel, pb=pb: e.matmul(PS[:, pb + j, :], lhsT=sel, rhs=xn2b[:, j * 512:(j + 1) * 512], start=True, stop=True),
                             reads=['xn2b', 'identb'], writes=['ps%d' % (pb + j)])
                    S.op('dve', lambda e, i=i, n=n, pb=pb: e.scalar_tensor_tensor(out=junk[:], in0=U[i][:], scalar=1.0,
                                                                         in1=PS[:, pb:pb + 4, :].rearrange("p a b -> p (a b)"), op0=ALU.mult, op1=ALU.mult,
                                                                         accum_out=actT[:, n:n + 1]),
                         reads=['U%d' % i] + ['ps%d' % (pb + j) for j in range(4)], writes=['B2', ak])
                S.op('act', lambda e: e.activation(out=actT[:], in_=actT[:], func=AF.Gelu), reads=[ak], writes=[ak])
                S.op('dve', lambda e: e.tensor_tensor(out=actT[:], in0=actT[:], in1=gatT[:], op=ALU.mult), reads=[ak, gk], writes=[ak])
            def tile_p2(ot, deferred):
                samp = ot == 8
                nr = NS if samp else 128
                orow = ot * 128
                tt = ot + 8
                H = Hs[ot % 2]; hk = 'H%d' % (ot % 2)
                eidT = eidTs[ot % 2]; ek = 'eidT%d' % (ot % 2)
                gatT = gatTs[ot % 2]; gk = 'gatT%d' % (ot % 2)
                actT = actTs[ot % 2]; ak = 'actT%d' % (ot % 2)
                kper = (len(deferred) + nr - 1) // nr + 1
                for n in range(nr):
                    i = n % NU
                    S.gather(U[i][:, :], v_tab[:, :], eidT[:, n:n + 1], NEXP, reads=[ek], writes=['U%d' % i])
                    vb_ = Vb[n % 2]; vbk = 'Vb%d' % (n % 2)
                    S.op('act', lambda e, i=i, vb_=vb_: e.activation(out=vb_[:], in_=U[i][:], func=AF.Copy), reads=['U%d' % i], writes=[vbk])
                    L = Ln[n % 4]; lk = 'Ln%d' % (n % 4)
                    S.op('dve', lambda e, L=L, n=n: e.scalar_tensor_tensor(out=L[:], in0=io_f[:, 0:128], scalar=float(n), in1=actT[:],
                                                                         op0=ALU.is_equal, op1=ALU.mult), reads=['io_f', ak], writes=[lk])
                    for j in range(4):
                        S.op('pe', lambda e, j=j, L=L, vb_=vb_, n=n: e.matmul(PS[:, 4 + j, :], lhsT=L[:], rhs=vb_[:, j * 512:(j + 1) * 512],
                                                                     start=(n == 0), stop=(n == nr - 1)),
                             reads=[lk, vbk], writes=['ps%d' % (4 + j)])
                    for _ in range(kper):
                        if deferred:
                            S.replay(deferred.pop(0))
                while deferred:
                    S.replay(deferred.pop(0))
                S.op('dve', lambda e: e.tensor_tensor(out=H[:], in0=H[:], in1=PS[:, 4:8, :].rearrange("p a b -> p (a b)"), op=ALU.add),
                     reads=[hk, 'ps4', 'ps5', 'ps6', 'ps7'], writes=[hk])
                rmsnorm(H, gf, hk, 'gf', ss, B2, outt=B1, okey='B1', jkey='B2')
                if samp:
                    S.dma('sp', y_s[:, :], B1[0:NS, :], reads=['B1'], writes=['o_y_s'])
                else:
                    S.dma('sp', y_p[orow:orow + 128, :], B1[:], reads=['B1'], writes=['o_y_p%d' % ot])
            tile_front(0)
            for ot in range(9):
                tile_p1(ot)
                deferred = []
                if ot + 1 < 9:
                    S.rec = deferred
                    tile_front(ot + 1)
                    S.rec = None
                tile_p2(ot, deferred)
        S.finish('sp')
    return nc


_OUT_NAMES = ["y_p", "y_s", "wkv_p", "shift_p", "chunkv_p", "wkv_s", "shift_s", "chunkv_s"]


def make_in_maps(inp):
    f = lambda a: np.ascontiguousarray(np.asarray(a), dtype=np.float32)
    x_prompt = f(inp["x_prompt"]); x_sample = f(inp["x_sample"])
    ws = f(inp["ws"])[0]; bs = f(inp["bs"])[0]
    wsT = np.ascontiguousarray(ws.transpose(2, 0, 1))
    bsT = np.ascontiguousarray(bs.T)
    wsTs = np.zeros((128, 8, 128), np.float32)
    for s in range(4):
        for t in range(4):
            for b in range(16):
                wsTs[s * 16 + b, :, t * 16 + b] = ws[:, t, s]
    bsTs = np.zeros((128, 8), np.float32)
    bsTs[:64] = np.repeat(bs[:, :4].T, 16, axis=0)
    shared = {
        "norm1_g": f(inp["norm1_g"])[0], "norm2_g": f(inp["norm2_g"])[0], "final_g": f(inp["final_g"]),
        "w_in": f(inp["w_in"])[0], "w_out": f(inp["w_out"])[0], "w_q": f(inp["w_q"])[0],
        "ln_g": f(inp["ln_v_g"])[0], "ln_b": f(inp["ln_v_b"])[0],
        "wsT": wsT, "bsT": bsT, "wsTs": wsTs, "bsTs": bsTs,
        "mu": f(inp["mu"])[0], "w0": f(inp["w0"])[0], "a0": f(inp["a0"])[0],
        "wa_up": np.ascontiguousarray(np.concatenate([f(inp["w_up"])[0], f(inp["a_up"])[0]], axis=0)),
        "g_up": f(inp["g_up"])[0],
        "k_k": f(inp["k_k"])[0], "k_a": f(inp["k_a"])[0], "r_k": f(inp["r_k"])[0].reshape(1024),
        "gn_g": f(inp["gn_g"])[0], "gn_b": f(inp["gn_b"])[0],
        "skT": np.ascontiguousarray(f(inp["sub_keys"])[0].transpose(2, 0, 1)),
        "u_tab": f(inp["u_tab"])[0], "v_tab": f(inp["v_tab"])[0],
    }
    state_wkv = f(inp["state_wkv"])[0]; state_shift = f(inp["state_shift"])[0]
    maps = []
    for c in range(8):
        b, half = c // 2, c % 2
        if half == 1:
            xseq = x_prompt[b]
        else:
            xseq = np.concatenate([np.zeros((1024, D), np.float32), x_prompt[b, :1024]], axis=0)
        xs = x_sample[16 * c:16 * c + 16].transpose(1, 0, 2).reshape(64, D)
        m = dict(shared)
        m["xseq"] = np.ascontiguousarray(xseq)
        m["xs"] = np.ascontiguousarray(xs)
        m["wkv0"] = np.ascontiguousarray(state_wkv[16 * c:16 * c + 16].reshape(128, 8192))
        m["shift0"] = np.ascontiguousarray(state_shift[16 * c:16 * c + 16])
        maps.append(m)
    return maps


def assemble(results):
    y_prompt = np.zeros((4, 2048, D), np.float32)
    y_sample = np.zeros((128, 4, D), np.float32)
    wkv_p = np.zeros((1, 4, 16, 64, 64), np.float32)
    shift_p = np.zeros((1, 4, D), np.float32)
    chunkv_p = np.zeros((1, 4, 128, 1024), np.float32)
    wkv_s = np.zeros((1, 128, 16, 64, 64), np.float32)
    shift_s = np.zeros((1, 128, D), np.float32)
    chunkv_s = np.zeros((1, 128, 4, 1024), np.float32)
    for c in range(8):
        r = results[c]
        b, half = c // 2, c % 2
        y_prompt[b, half * 1024:(half + 1) * 1024] = r["y_p"]
        y_sample[16 * c:16 * c + 16] = r["y_s"].reshape(4, 16, D).transpose(1, 0, 2)
        if half == 1:
            wkv_p[0, b] = r["wkv_p"].reshape(16, 64, 64)
            shift_p[0, b] = r["shift_p"][0]
            chunkv_p[0, b] = r["chunkv_p"]
        wkv_s[0, 16 * c:16 * c + 16] = r["wkv_s"].reshape(16, 16, 64, 64)
        shift_s[0, 16 * c:16 * c + 16] = r["shift_s"]
        chunkv_s[0, 16 * c:16 * c + 16] = r["chunkv_s"].reshape(4, 16, 1024).transpose(1, 0, 2)
    return (y_prompt, y_sample, wkv_p, shift_p, chunkv_p, wkv_s, shift_s, chunkv_s)


def kernel(**inputs):
    nc = build()
    maps = make_in_maps(inputs)
    res = run_bass_kernel_spmd(nc, maps, core_ids=list(range(8)))
    return assemble(res.results)
```

```python
import math, os
from contextlib import ExitStack
import numpy as np
import concourse.bass as bass
import concourse.mybir as mybir
from concourse.bass_utils import run_bass_kernel_spmd

F32 = mybir.dt.float32
BF16 = mybir.dt.bfloat16
I32 = mybir.dt.int32
U32 = mybir.dt.uint32
AF = mybir.ActivationFunctionType
ALU = mybir.AluOpType
AX = mybir.AxisListType

D = 2048
WIN = 5408
WR = 3360
NT = 2048
NOWN = 1024
NS = 64
NEXP = 16384
RMS_EPS = 1e-6
LN_EPS = 1e-5
GN_EPS = 6.4e-4


class Sched:
    def __init__(self, nc, es):
        self.nc = nc
        self.eng = {'pe': nc.tensor, 'dve': nc.vector, 'act': nc.scalar, 'pool': nc.gpsimd, 'sp': nc.sync}
        self.sem = {k: es.enter_context(nc.semaphore('s_' + k)) for k in self.eng}
        self.cnt = {k: 0 for k in self.eng}
        self.NDS = 96
        self.dsem = [es.enter_context(nc.semaphore('d%d' % i)) for i in range(self.NDS)]
        self.dcnt = [0] * self.NDS
        self.dnext = 0
        self.dnext_sw = 0
        self.NHW = 72
        self.waited = {k: {} for k in self.eng}
        self.lastw = {}
        self.readers = {}
        self.rec = None

    def _wait(self, e, tok):
        if tok is None:
            return
        key, val = tok
        if self.waited[e].get(key, 0) >= val:
            return
        self.waited[e][key] = val
        sem = self.sem[key] if isinstance(key, str) else self.dsem[key]
        self.eng[e].wait_ge(sem, val)

    def _deps(self, e, reads, writes):
        for r in reads:
            self._wait(e, self.lastw.get(r))
        for w in writes:
            self._wait(e, self.lastw.get(w))
            for tok in self.readers.get(w, ()):
                self._wait(e, tok)

    def _commit(self, tok, reads, writes):
        for r in reads:
            if r in self.lastw:
                self.readers.setdefault(r, []).append(tok)
        for w in writes:
            self.lastw[w] = tok
            self.readers[w] = []

    def op(self, e, fn, reads=(), writes=()):
        if self.rec is not None:
            self.rec.append(('op', e, fn, tuple(reads), tuple(writes)))
            return
        self._deps(e, reads, writes)
        ins = fn(self.eng[e])
        self.cnt[e] += 1
        ins.then_inc(self.sem[e], 1)
        self._commit((e, self.cnt[e]), reads, writes)

    def _dtok(self, sw=False):
        if sw:
            j = self.NHW + self.dnext_sw
            self.dnext_sw = (self.dnext_sw + 1) % (self.NDS - self.NHW)
        else:
            j = self.dnext
            self.dnext = (self.dnext + 1) % self.NHW
        self.dcnt[j] += 16
        return j

    def dma(self, q, out, in_, reads=(), writes=(), **kw):
        if self.rec is not None:
            self.rec.append(('dma', q, out, in_, tuple(reads), tuple(writes), kw))
            return
        self._deps(q, reads, writes)
        j = self._dtok(sw=(q == 'pool'))
        if self.dcnt[j] > 16:
            self._wait(q, (j, self.dcnt[j] - 16))
        self.eng[q].dma_start(out=out, in_=in_, **kw).then_inc(self.dsem[j], 16)
        self._commit((j, self.dcnt[j]), reads, writes)

    def gather(self, out, table, idx, nrows, reads=(), writes=()):
        self._deps('pool', reads, writes)
        j = self._dtok(sw=True)
        if self.dcnt[j] > 16:
            self._wait('pool', (j, self.dcnt[j] - 16))
        if getattr(self, 'breg', None) is None:
            self.breg = self.nc.gpsimd.to_reg(nrows - 1)
        self.nc.gpsimd.indirect_dma_start(
            out=out, out_offset=None, in_=table,
            in_offset=bass.IndirectOffsetOnAxis(ap=idx, axis=0),
            bounds_check=self.breg, oob_is_err=False).then_inc(self.dsem[j], 16)
        self._commit((j, self.dcnt[j]), reads, writes)

    def replay(self, item):
        if item[0] == 'op':
            self.op(item[1], item[2], item[3], item[4])
        else:
            self.dma(item[1], item[2], item[3], item[4], item[5], **item[6])

    def barrier(self):
        for e in self.eng:
            for k in self.eng:
                if self.cnt[k] > 0:
                    self._wait(e, (k, self.cnt[k]))
            for j in range(self.NDS):
                if self.dcnt[j] > 0:
                    self._wait(e, (j, self.dcnt[j]))

    def finish(self, e='sp'):
        for k in list(self.lastw):
            self._wait(e, self.lastw[k])


def bc_rows(ap1d, n, parts=128):
    return bass.AP(tensor=ap1d.tensor, offset=ap1d.offset, ap=[[0, parts], [1, n]])


def build(debug=False):
    nc = bass.Bass("TRN2", target_bir_lowering=False)

    def din(name, shape, dt=F32):
        return nc.dram_tensor(name, list(shape), dt, kind="ExternalInput").ap()

    def dout(name, shape, dt=F32):
        return nc.dram_tensor(name, list(shape), dt, kind="ExternalOutput").ap()

    def dscr(name, shape, dt=F32):
        if debug:
            return nc.dram_tensor(name, list(shape), dt, kind="ExternalOutput").ap()
        return nc.dram_tensor(name, list(shape), dt).ap()

    xseq = din("xseq", [NT, D])
    xs = din("xs", [NS, D])
    wkv0 = din("wkv0", [128, 8192])
    shift0 = din("shift0", [16, D])
    norm1_g = din("norm1_g", [D]); norm2_g = din("norm2_g", [D]); final_g = din("final_g", [D])
    w_in = din("w_in", [D, WIN]); w_out = din("w_out", [D, D]); w_q = din("w_q", [D, D])
    ln_g = din("ln_g", [1024]); ln_b = din("ln_b", [1024])
    wsT = din("wsT", [128, 8, 128]); bsT = din("bsT", [128, 8])
    wsTs = din("wsTs", [128, 8, 128]); bsTs = din("bsTs", [128, 8])
    mu = din("mu", [WR]); w0 = din("w0", [1024]); a0 = din("a0", [1024])
    wa_up = din("wa_up", [128, 1024]); g_up = din("g_up", [160, 1024])
    k_k = din("k_k", [1024]); k_a = din("k_a", [1024]); r_k = din("r_k", [1024])
    gn_g = din("gn_g", [1024]); gn_b = din("gn_b", [1024])
    skT = din("skT", [128, 2, 128])
    u_tab = din("u_tab", [NEXP, D]); v_tab = din("v_tab", [NEXP, D])

    y_p = dout("y_p", [NOWN, D]); y_s = dout("y_s", [NS, D])
    wkv_p = dout("wkv_p", [128, 512]); shift_p = dout("shift_p", [1, D]); chunkv_p = dout("chunkv_p", [128, 1024])
    wkv_s = dout("wkv_s", [128, 8192]); shift_s = dout("shift_s", [16, D]); chunkv_s = dout("chunkv_s", [NS, 1024])

    projP = dscr("projP", [NT + 1, WIN])
    projS = dscr("projS", [80, WIN])
    QN = ["r", "w", "k", "a", "b"]
    scP = {q: dscr("scP_" + q, [NT, 1024]) for q in QN + ["v"]}
    scS = {q: dscr("scS_" + q, [NS, 1024]) for q in QN + ["v"]}
    yscP = dscr("yscP", [NT, 1024]); yscS = dscr("yscS", [NS, 1024])
    gP = dscr("gP", [NOWN + NS, 1024]); bonP = dscr("bonP", [NOWN + NS, 1024]); yaP = dscr("yaP", [NOWN + NS, 1024])

    with ExitStack() as es:
        S = Sched(nc, es)

        _nm = {'n': 0}

        def sbuf(st, name, shape, dt=F32):
            _nm['n'] += 1
            return st.enter_context(nc.sbuf_tensor("%s_%d" % (name, _nm['n']), list(shape), dt))

        PS = es.enter_context(nc.psum_tensor("PS", [128, 8, 512], F32))
        io_fp = sbuf(es, "io_fp", [128, 128])
        io_f = sbuf(es, "io_f", [128, 256])
        ident = sbuf(es, "ident", [128, 128])
        identb = sbuf(es, "identb", [128, 128], BF16)
        cmask = sbuf(es, "cmask", [128, 128])
        zero_t = sbuf(es, "zero_t", [128, 512])
        S.op('pool', lambda e: e.iota(io_fp[:], pattern=[[1, 128]], base=0, channel_multiplier=-1,
                                      allow_small_or_imprecise_dtypes=True), writes=['io_fp'])
        S.op('pool', lambda e: e.iota(io_f[:], pattern=[[1, 256]], base=0, channel_multiplier=0,
                                      allow_small_or_imprecise_dtypes=True), writes=['io_f'])
        S.op('dve', lambda e: e.tensor_scalar(out=ident[:], in0=io_fp[:], scalar1=0.0, scalar2=None, op0=ALU.is_equal),
             reads=['io_fp'], writes=['ident'])
        S.op('dve', lambda e: e.tensor_copy(out=identb[:], in_=ident[:]), reads=['ident'], writes=['identb'])
        S.op('dve', lambda e: e.tensor_scalar(out=cmask[:], in0=io_fp[:], scalar1=0.0, scalar2=None, op0=ALU.is_ge),
             reads=['io_fp'], writes=['cmask'])
        S.op('dve', lambda e: e.memset(zero_t[:], 0.0), writes=['zero_t'])

        rr = {'ev': 0}

        def evac_eng():
            rr['ev'] += 1
            return 'act' if rr['ev'] % 2 else 'dve'

        def copy(e, out, in_, reads, writes):
            if e == 'act':
                S.op('act', lambda g: g.activation(out=out, in_=in_, func=AF.Copy), reads=reads, writes=writes)
            else:
                S.op(e, lambda g: g.tensor_copy(out=out, in_=in_), reads=reads, writes=writes)

        def rstd_from_ss(ss, n, eps, tag):
            S.op('act', lambda g: g.activation(out=ss, in_=ss, func=AF.Sqrt, bias=eps_tiles[eps][:, 0:1], scale=1.0 / n),
                 reads=[tag, 'epsc'], writes=[tag])
            S.op('dve', lambda g: g.reciprocal(out=ss, in_=ss), reads=[tag], writes=[tag])

        eps_tiles = {}
        for ev in (RMS_EPS, LN_EPS, GN_EPS, 0.0):
            t = sbuf(es, "eps%d" % len(eps_tiles), [128, 1])
            S.op('dve', lambda e, t=t, ev=ev: e.memset(t[:], ev), writes=['epsc'])
            eps_tiles[ev] = t

        def transpose_to(dst_fn, src_tile, nblk, skey, dkey, psb=(0, 1)):
            for g0 in range(0, nblk, 4):
                nb = min(4, nblk - g0)
                bank = psb[(g0 // 4) % len(psb)]
                for i in range(nb):
                    c = g0 + i
                    S.op('pe', lambda e, c=c, i=i, bank=bank: e.transpose(out=PS[:, bank, i * 128:(i + 1) * 128],
                                                                    in_=src_tile[:, c * 128:(c + 1) * 128], identity=ident[:]),
                         reads=[skey, 'ident'], writes=['ps%d' % bank])
                copy(evac_eng(), dst_fn(g0, nb), PS[:, bank, 0:nb * 128].rearrange("p (a b) -> p a b", a=nb),
                     ['ps%d' % bank], [dkey])

        def rmsnorm(xt, gt, xkey, gkey, ss, junk, outt=None, okey=None, jkey='junk'):
            outt = xt if outt is None else outt
            okey = xkey if okey is None else okey
            S.op('dve', lambda e: e.memset(ss[:, 0:1], 0.0), writes=['ss'])
            S.op('act', lambda e: e.activation(out=junk[:], in_=xt[:], func=AF.Square, accum_out=ss[:, 0:1]),
                 reads=[xkey, 'ss'], writes=[jkey, 'ss'])
            rstd_from_ss(ss[:, 0:1], D, RMS_EPS, 'ss')
            S.op('dve', lambda e: e.scalar_tensor_tensor(out=outt[:], in0=xt[:], scalar=ss[:, 0:1], in1=gt[:],
                                                         op0=ALU.mult, op1=ALU.mult),
                 reads=[xkey, 'ss', gkey], writes=[okey])

        NTT = 17
        with ExitStack() as p1:
            xnT = sbuf(p1, "xnT", [128, 16, NTT * 128], BF16)
            g1 = sbuf(p1, "g1", [128, D])
            xt2 = [sbuf(p1, "xt%d" % i, [128, D]) for i in range(2)]
            junk = sbuf(p1, "junk", [128, D])
            xnb = sbuf(p1, "xnb", [128, D], BF16)
            ss = sbuf(p1, "ss", [128, 4])
            wb = [sbuf(p1, "wb%d" % i, [128, 16, 512], BF16) for i in range(2)]
            stg = [sbuf(p1, "stg%d" % i, [128, 512]) for i in range(4)]
            S.dma('sp', g1[:], bc_rows(norm1_g, D), writes=['g1'])
            S.dma('sp', projP[0:1, 0:5120].rearrange("o (a b) -> (o a) b", a=10), zero_t[0:10, :], reads=['zero_t'], writes=['projP_z'])
            S.dma('sp', projP[0:1, 5120:WIN], zero_t[0:1, 0:WIN - 5120], reads=['zero_t'], writes=['projP_z'])
            for tt in range(NTT):
                xt = xt2[tt % 2]
                xk = 'xt%d' % (tt % 2)
                if tt < 16:
                    S.dma('sp', xt[:], xseq[tt * 128:(tt + 1) * 128, :], writes=[xk])
                else:
                    S.op('pool', lambda e: e.memset(xt[:], 0.0), writes=[xk])
                    S.dma('sp', xt[0:NS, :], xs[:, :], writes=[xk])
                rmsnorm(xt, g1, xk, 'g1', ss, junk)
                if tt == 15:
                    S.dma('sp', shift_p[0:1, :], xt[127:128, :], reads=[xk], writes=['o_shift_p'])
                if tt == 16:
                    S.dma('sp', shift_s[:, :], xt[48:64, :], reads=[xk], writes=['o_shift_s'])
                    S.dma('sp', xt[64:80, :], shift0[:, :], reads=[], writes=[xk])
                S.op('act', lambda e, xt=xt: e.activation(out=xnb[:], in_=xt[:], func=AF.Copy), reads=[xk], writes=['xnb'])
                for g0 in (0, 8):
                    bank = (g0 // 8) % 2
                    psb_ = PS[:, bank, :].bitcast(BF16)
                    for c in range(8):
                        S.op('pe', lambda e, c=c, g0=g0, psb_=psb_: e.transpose(out=psb_[:, c * 128:(c + 1) * 128], in_=xnb[:, (g0 + c) * 128:(g0 + c + 1) * 128], identity=identb[:]),
                             reads=['xnb', 'identb'], writes=['ps%d' % bank])
                    copy(evac_eng(), xnT[:, g0:g0 + 8, tt * 128:(tt + 1) * 128], psb_.rearrange("p (a b) -> p a b", a=8), ['ps%d' % bank], ['xnT'])
            ncb = (WIN + 511) // 512
            ei = 0
            for cb in range(ncb):
                c0 = cb * 512
                cw = min(512, WIN - c0)
                w = wb[cb % 2]
                wk = 'wb%d' % (cb % 2)
                S.dma('pool', w[:, :, 0:cw], w_in.rearrange("(a p) c -> p a c", p=128)[:, :, c0:c0 + cw], writes=[wk])
                for tt in range(NTT):
                    if tt < 8 and c0 < 2048:
                        continue
                    bank = 2 + (ei % 2)
                    for dc in range(16):
                        S.op('pe', lambda e, dc=dc, bank=bank, tt=tt, w=w, cw=cw: e.matmul(
                            PS[:, bank, 0:cw], lhsT=xnT[:, dc, tt * 128:(tt + 1) * 128], rhs=w[:, dc, 0:cw],
                            start=(dc == 0), stop=(dc == 15)), reads=['xnT', wk], writes=['ps%d' % bank])
                    st = stg[ei % 4]
                    sk = 'stg%d' % (ei % 4)
                    copy(evac_eng(), st[:, 0:cw], PS[:, bank, 0:cw], ['ps%d' % bank], [sk])
                    if tt < 16:
                        S.dma('sp', projP[1 + tt * 128:1 + (tt + 1) * 128, c0:c0 + cw], st[:, 0:cw], reads=[sk],
                              writes=['projP_%d_%d' % (tt, cb)])
                    else:
                        S.dma('sp', projS[16:80, c0:c0 + cw], st[0:64, 0:cw], reads=[sk], writes=['projS_a%d' % cb])
                        S.dma('sp', projS[0:16, c0:c0 + cw], st[64:80, 0:cw], reads=[sk], writes=['projS_b%d' % cb])
                    ei += 1
        S.barrier()
        proj_keys = lambda tt: (['projP_%d_%d' % (t2, cb) for t2 in (tt, tt - 1) if t2 >= 0 for cb in range(11)
                                 if not (t2 < 8 and cb < 4)] + ['projP_z']) if tt < 16 else \
            (['projS_a%d' % cb for cb in range(11)] + ['projS_b%d' % cb for cb in range(11)])

        with ExitStack() as p2:
            cst = {}
            for nm, src, n in (("mu", mu, WR), ("lng", ln_g, 1024), ("lnb", ln_b, 1024), ("w0", w0, 1024), ("a0", a0, 1024),
                               ("kk", k_k, 1024), ("ka", k_a, 1024), ("rk", r_k, 1024)):
                cst[nm] = sbuf(p2, "c_" + nm, [128, n])
                S.dma('sp', cst[nm][:], bc_rows(src, n), writes=['c_' + nm])
            waup = sbuf(p2, "waup", [128, 1024]); gup1 = sbuf(p2, "gup1", [128, 1024]); gup2 = sbuf(p2, "gup2", [32, 1024])
            S.dma('sp', waup[:], wa_up[:, :], writes=['waup'])
            S.dma('sp', gup1[:], g_up[0:128, :], writes=['gup1'])
            S.dma('sp', gup2[:], g_up[128:160, :], writes=['gup2'])
            wsf = sbuf(p2, "wsf", [128, 8, 128]); wsb = sbuf(p2, "wsb", [128, 8, 128], BF16); wsbs = sbuf(p2, "wsbs", [128, 8, 128], BF16)
            bst = sbuf(p2, "bst", [128, 8]); bsts = sbuf(p2, "bsts", [128, 8])
            S.dma('sp', bst[:], bsT[:, :], writes=['bst'])
            S.dma('sp', bsts[:], bsTs[:, :], writes=['bsts'])
            for srcw, dstw, key in ((wsT, wsb, 'wsb'), (wsTs, wsbs, 'wsbs')):
                S.dma('sp', wsf[:], srcw[:, :, :], writes=['wsf'])
                S.op('dve', lambda e, dstw=dstw: e.tensor_tensor(out=dstw[:], in0=wsf[:],
                                                                in1=cmask[:].unsqueeze(1).to_broadcast([128, 8, 128]), op=ALU.mult),
                     reads=['wsf', 'cmask'], writes=[key])
            pz = sbuf(p2, "pz", [128, 2048]); vb = sbuf(p2, "vb", [128, 1024], BF16)
            q = sbuf(p2, "q", [128, WR]); pp = sbuf(p2, "pp", [128, WR])
            ss = sbuf(p2, "ss", [128, 4]); st16 = sbuf(p2, "st16", [128, 16]); st16b = sbuf(p2, "st16b", [128, 16])
            lT1 = sbuf(p2, "lT1", [128, 128]); lT2 = sbuf(p2, "lT2", [128, 128]); lT3 = sbuf(p2, "lT3", [32, 128])
            W = sbuf(p2, "W", [128, 1024]); A = sbuf(p2, "A", [128, 1024]); KK = sbuf(p2, "KK", [128, 1024])
            K2 = sbuf(p2, "K2", [128, 1024]); AS = sbuf(p2, "AS", [128, 1024]); BS = sbuf(p2, "BS", [128, 1024])
            T = sbuf(p2, "T", [128, 1024]); G = sbuf(p2, "G", [128, 1024]); ya = sbuf(p2, "ya", [128, 1024])
            S.op('pool', lambda e: e.memset(pz[:], 0.0), writes=['pz'])
            S.op('pool', lambda e: e.memset(q[:], 0.0), writes=['q'])
            S.op('pool', lambda e: e.memset(pp[:], 0.0), writes=['pp'])
            v3 = lambda t: t[:].rearrange("p (h j) -> p h j", h=16)
            bc16 = lambda t: t[:, 0:16].unsqueeze(2).to_broadcast([128, 16, 64])
            for tt in range(NTT):
                own = tt >= 8
                samp = tt == 16
                nr = NS if samp else 128
                pk = proj_keys(tt)
                orow = (tt - 8) * 128
                if own:
                    if samp:
                        S.dma('sp', pz[0:NS, :], projS[16:80, 0:2048], reads=pk, writes=['pz'])
                    else:
                        S.dma('sp', pz[:], projP[1 + tt * 128:1 + (tt + 1) * 128, 0:2048], reads=pk, writes=['pz'])
                    S.op('act', lambda e: e.activation(out=pz[:], in_=pz[:], func=AF.Gelu), reads=['pz'], writes=['pz'])
                    vv = pz[:, 1024:2048]
                    S.op('dve', lambda e: e.tensor_reduce(out=ss[:, 0:1], in_=vv, axis=AX.X, op=ALU.add), reads=['pz'], writes=['ss'])
                    S.op('dve', lambda e: e.tensor_scalar(out=ss[:, 0:1], in0=ss[:, 0:1], scalar1=-1.0 / 1024, scalar2=None, op0=ALU.mult),
                         reads=['ss'], writes=['ss'])
                    S.op('dve', lambda e: e.tensor_scalar(out=vv, in0=vv, scalar1=ss[:, 0:1], scalar2=None, op0=ALU.add),
                         reads=['pz', 'ss'], writes=['pz'])
                    S.op('dve', lambda e: e.memset(ss[:, 1:2], 0.0), writes=['ss1'])
                    S.op('act', lambda e: e.activation(out=T[:], in_=vv, func=AF.Square, accum_out=ss[:, 1:2]),
                         reads=['pz', 'ss1'], writes=['T', 'ss1'])
                    rstd_from_ss(ss[:, 1:2], 1024, LN_EPS, 'ss1')
                    S.op('dve', lambda e: e.scalar_tensor_tensor(out=vv, in0=vv, scalar=ss[:, 1:2], in1=cst['lng'][:],
                                                                 op0=ALU.mult, op1=ALU.mult), reads=['pz', 'ss1', 'c_lng'], writes=['pz'])
                    S.op('dve', lambda e: e.tensor_tensor(out=vv, in0=vv, in1=cst['lnb'][:], op=ALU.add), reads=['pz', 'c_lnb'], writes=['pz'])
                    if tt == 15:
                        S.dma('sp', chunkv_p[:, :], vv, reads=['pz'], writes=['o_chunkv_p'])
                    if samp:
                        S.dma('sp', chunkv_s[:, :], pz[0:NS, 1024:2048], reads=['pz'], writes=['o_chunkv_s'])
                        S.op('dve', lambda e: e.memset(pz[64:128, 1024:2048], 0.0), reads=['pz'], writes=['pz'])
                    S.op('dve', lambda e: e.tensor_copy(out=vb[:], in_=vv), reads=['pz'], writes=['vb'])
                    wmat = wsbs if samp else wsb
                    wkey = 'wsbs' if samp else 'wsb'
                    for h in range(8):
                        bank = h // 4
                        S.op('pe', lambda e, h=h, bank=bank: e.matmul(PS[:, bank, (h % 4) * 128:(h % 4 + 1) * 128], lhsT=wmat[:, h, :],
                                                                     rhs=vb[:, h * 128:(h + 1) * 128], start=True, stop=True),
                             reads=[wkey, 'vb'], writes=['ps%d' % bank])
                    bt = bsts if samp else bst
                    for h in range(8):
                        bank = h // 4
                        S.op('dve', lambda e, h=h, bank=bank: e.scalar_tensor_tensor(
                            out=ya[:, h * 128:(h + 1) * 128], in0=PS[:, bank, (h % 4) * 128:(h % 4 + 1) * 128], scalar=bt[:, h:h + 1],
                            in1=pz[:, h * 128:(h + 1) * 128], op0=ALU.add, op1=ALU.mult),
                             reads=['ps%d' % bank, 'pz', 'bst', 'bsts'], writes=['ya'])
                    S.dma('sp', yaP[orow:orow + nr, :], ya[0:nr, :], reads=['ya'], writes=['yaP%d' % tt])
                if samp:
                    S.dma('sp', q[0:NS, :], projS[16:80, 2048:WIN], reads=pk, writes=['q'])
                    S.dma('sp', pp[0:NS, :], projS[0:64, 2048:WIN], reads=pk, writes=['pp'])
                else:
                    S.dma('sp', q[:], projP[1 + tt * 128:1 + (tt + 1) * 128, 2048:WIN], reads=pk, writes=['q'])
                    S.dma('sp', pp[:], projP[tt * 128:(tt + 1) * 128, 2048:WIN], reads=pk, writes=['pp'])
                S.op('dve', lambda e: e.tensor_tensor(out=pp[:], in0=pp[:], in1=q[:], op=ALU.subtract), reads=['pp', 'q'], writes=['pp'])
                S.op('pool', lambda e: e.tensor_tensor(out=pp[:], in0=pp[:], in1=cst['mu'][:], op=ALU.mult), reads=['pp', 'c_mu'], writes=['pp'])
                S.op('dve', lambda e: e.tensor_tensor(out=q[:], in0=q[:], in1=pp[:], op=ALU.add), reads=['pp', 'q'], writes=['q'])
                r_ = q[:, 0:1024]; k_ = q[:, 1024:2048]; v_ = q[:, 2048:3072]
                S.op('act', lambda e: e.activation(out=q[:, 3072:3136], in_=q[:, 3072:3136], func=AF.Tanh), reads=['q'], writes=['q'])
                S.op('act', lambda e: e.activation(out=q[:, 3200:3360], in_=q[:, 3200:3360], func=AF.Sigmoid), reads=['q'], writes=['q'])
                for (c0, cw, dst, dk) in ((3072, 128, lT1, 'lT1'), (3200, 128, lT2, 'lT2'), (3328, 32, lT3, 'lT3')):
                    S.op('pe', lambda e, c0=c0, cw=cw: e.transpose(out=PS[0:cw, 0, 0:128], in_=q[:, c0:c0 + cw], identity=ident[:]),
                         reads=['q', 'ident'], writes=['ps0'])
                    copy('dve', dst[0:cw, :], PS[0:cw, 0, 0:128], ['ps0'], [dk])
                for j in range(2):
                    cs = slice(j * 512, (j + 1) * 512)
                    S.op('pe', lambda e, j=j, cs=cs: e.matmul(PS[:, 2 + j, :], lhsT=lT1[0:64, :], rhs=waup[0:64, cs], start=True, stop=True),
                         reads=['lT1', 'waup'], writes=['ps%d' % (2 + j)])
                    S.op('pe', lambda e, j=j, cs=cs: e.matmul(PS[:, 4 + j, :], lhsT=lT1[64:128, :], rhs=waup[64:128, cs], start=True, stop=True),
                         reads=['lT1', 'waup'], writes=['ps%d' % (4 + j)])
                    S.op('pe', lambda e, j=j, cs=cs: e.matmul(PS[:, 6 + j, :], lhsT=lT2[:, :], rhs=gup1[:, cs], start=True, stop=False),
                         reads=['lT2', 'gup1'], writes=['ps%d' % (6 + j)])
                    S.op('pe', lambda e, j=j, cs=cs: e.matmul(PS[:, 6 + j, :], lhsT=lT3[0:32, :], rhs=gup2[0:32, cs], start=False, stop=True),
                         reads=['lT3', 'gup2'], writes=['ps%d' % (6 + j)])
                ps2 = lambda b: PS[:, b:b + 2, :].rearrange("p a b -> p (a b)")
                for j in range(2):
                    cs = slice(j * 512, (j + 1) * 512)
                    S.op('dve', lambda e, j=j, cs=cs: e.tensor_tensor(out=W[:, cs], in0=PS[:, 2 + j, :], in1=cst['w0'][:, cs], op=ALU.add),
                         reads=['ps%d' % (2 + j), 'c_w0'], writes=['W'])
                    S.op('dve', lambda e, j=j, cs=cs: e.tensor_tensor(out=A[:, cs], in0=PS[:, 4 + j, :], in1=cst['a0'][:, cs], op=ALU.add),
                         reads=['ps%d' % (4 + j), 'c_a0'], writes=['A'])
                    copy('act', G[:, cs], PS[:, 6 + j, :], ['ps%d' % (6 + j)], ['G'])
                S.op('act', lambda e: e.activation(out=W[:], in_=W[:], func=AF.Sigmoid), reads=['W'], writes=['W'])
                if samp:
                    S.op('act', lambda e: e.activation(out=W[:], in_=W[:], func=AF.Exp, scale=-math.exp(-0.5), bias=eps_tiles[0.0][:, 0:1]), reads=['W', 'epsc'], writes=['W'])
                else:
                    S.op('act', lambda e: e.activation(out=W[:], in_=W[:], func=AF.Copy, scale=-math.exp(-0.5)), reads=['W'], writes=['W'])
                S.op('act', lambda e: e.activation(out=A[:], in_=A[:], func=AF.Sigmoid), reads=['A'], writes=['A'])
                S.op('dve', lambda e: e.tensor_tensor(out=KK[:], in0=k_, in1=cst['kk'][:], op=ALU.mult), reads=['q', 'c_kk'], writes=['KK'])
                S.op('pool', lambda e: e.tensor_tensor(out=T[:], in0=KK[:], in1=KK[:], op=ALU.mult), reads=['KK'], writes=['T'])
                S.op('dve', lambda e: e.tensor_reduce(out=st16[:], in_=v3(T), axis=AX.X, op=ALU.add), reads=['T'], writes=['st16'])
                S.op('dve', lambda e: e.tensor_scalar(out=st16[:], in0=st16[:], scalar1=1e-24, scalar2=None, op0=ALU.max), reads=['st16'], writes=['st16'])
                rstd_from_ss(st16[:], 1.0, 0.0, 'st16')
                S.op('dve', lambda e: e.tensor_tensor(out=v3(KK), in0=v3(KK), in1=bc16(st16), op=ALU.mult), reads=['KK', 'st16'], writes=['KK'])
                S.op('dve', lambda e: e.scalar_tensor_tensor(out=T[:], in0=A[:], scalar=-1.0, in1=cst['ka'][:], op0=ALU.add, op1=ALU.mult),
                     reads=['A', 'c_ka'], writes=['T'])
                S.op('dve', lambda e: e.scalar_tensor_tensor(out=K2[:], in0=T[:], scalar=1.0, in1=k_, op0=ALU.add, op1=ALU.mult),
                     reads=['T', 'q'], writes=['K2'])
                S.op('pool', lambda e: e.tensor_tensor(out=BS[:], in0=KK[:], in1=A[:], op=ALU.mult), reads=['KK', 'A'], writes=['BS'])
                S.op('act', lambda e: e.activation(out=AS[:], in_=KK[:], func=AF.Copy, scale=-1.0), reads=['KK'], writes=['AS'])
                if own:
                    S.op('dve', lambda e: e.tensor_tensor(out=T[:], in0=r_, in1=K2[:], op=ALU.mult), reads=['q', 'K2'], writes=['T'])
                    S.op('dve', lambda e: e.tensor_tensor(out=T[:], in0=T[:], in1=cst['rk'][:], op=ALU.mult), reads=['T', 'c_rk'], writes=['T'])
                    S.op('dve', lambda e: e.tensor_reduce(out=st16b[:], in_=v3(T), axis=AX.X, op=ALU.add), reads=['T'], writes=['st16b'])
                    S.op('dve', lambda e: e.tensor_tensor(out=v3(T), in0=q[:, 2048:3072].rearrange("p (h j) -> p h j", h=16), in1=bc16(st16b), op=ALU.mult),
                         reads=['q', 'st16b'], writes=['T'])
                    S.dma('sp', bonP[orow:orow + nr, :], T[0:nr, :], reads=['T'], writes=['bonP%d' % tt])
                    S.dma('sp', gP[orow:orow + nr, :], G[0:nr, :], reads=['G'], writes=['gP%d' % tt])
                dst = scS if samp else scP
                r0 = 0 if samp else tt * 128
                for nm, src, key in (("r", r_, 'q'), ("w", W[:], 'W'), ("k", K2[:], 'K2'), ("a", AS[:], 'AS'), ("b", BS[:], 'BS'), ("v", v_, 'q')):
                    S.dma('sp', dst[nm][r0:r0 + nr, :], src[0:nr, :], reads=[key], writes=['sc_%s_%d' % (nm, tt)])

        S.barrier()

        def scan(p3, name, G_, I_, J_, nsteps, TB, load_j, load_v, store_y, state_init, state_out):
            GI = G_ * I_
            St = sbuf(p3, name + "_S", [128, G_, I_, J_])
            tmp = sbuf(p3, name + "_tmp", [128, G_, I_, J_])
            tmp2 = [sbuf(p3, name + "_tmp2_%d" % i, [128, G_, I_, J_]) for i in range(2)]
            sa = sbuf(p3, name + "_sa", [128, G_, I_])
            jv = [{qn: sbuf(p3, "%s_j%s%d" % (name, qn, i), [128, TB, G_, J_]) for qn in QN} for i in range(2)]
            iv = [sbuf(p3, "%s_iv%d" % (name, i), [128, TB, G_, I_]) for i in range(2)]
            yv = [sbuf(p3, "%s_yv%d" % (name, i), [128, TB, G_, I_]) for i in range(2)]
            sk = name + '_S'
            state_init(St, sk)
            nblk = nsteps // TB
            bcI = lambda ap: ap.unsqueeze(2).to_broadcast([128, G_, I_, J_])
            bcJ = lambda ap: ap.unsqueeze(3).to_broadcast([128, G_, I_, J_])

            def load_blk(b):
                i = b % 2
                for qn in QN:
                    load_j(qn, b, jv[i][qn], '%s_j%s%d' % (name, qn, i))
                load_v(b, iv[i], '%s_iv%d' % (name, i))
            load_blk(0)
            for b in range(nblk):
                i = b % 2
                if b + 1 < nblk:
                    load_blk(b + 1)
                jk = {qn: '%s_j%s%d' % (name, qn, i) for qn in QN}
                ivk = '%s_iv%d' % (name, i)
                yk = '%s_yv%d' % (name, i)
                for t in range(TB):
                    a_t = jv[i]['a'][:, t]; w_t = jv[i]['w'][:, t]; b_t = jv[i]['b'][:, t]; k_t = jv[i]['k'][:, t]; r_t = jv[i]['r'][:, t]
                    v_t = iv[i][:, t]
                    t2 = tmp2[t % 2]; t2k = '%s_t2_%d' % (name, t % 2)
                    S.op('pool', lambda e, t2=t2, v_t=v_t, k_t=k_t: e.tensor_tensor(out=t2[:], in0=bcJ(v_t), in1=bcI(k_t), op=ALU.mult),
                         reads=[ivk, jk['k']], writes=[t2k])
                    S.op('dve', lambda e, a_t=a_t: e.tensor_tensor(out=tmp[:], in0=St[:], in1=bcI(a_t), op=ALU.mult),
                         reads=[sk, jk['a']], writes=[name + '_tmp'])
                    S.op('dve', lambda e: e.tensor_reduce(out=sa[:], in_=tmp[:], axis=AX.X, op=ALU.add), reads=[name + '_tmp'], writes=[name + '_sa'])
                    S.op('dve', lambda e, w_t=w_t: e.tensor_tensor(out=St[:], in0=St[:], in1=bcI(w_t), op=ALU.mult),
                         reads=[sk, jk['w']], writes=[sk])
                    S.op('dve', lambda e, b_t=b_t: e.tensor_tensor(out=tmp[:], in0=bcJ(sa[:]), in1=bcI(b_t), op=ALU.mult),
                         reads=[name + '_sa', jk['b']], writes=[name + '_tmp'])
                    S.op('dve', lambda e: e.tensor_tensor(out=St[:], in0=St[:], in1=tmp[:], op=ALU.add), reads=[sk, name + '_tmp'], writes=[sk])
                    S.op('dve', lambda e, t2=t2: e.tensor_tensor(out=St[:], in0=St[:], in1=t2[:], op=ALU.add), reads=[sk, t2k], writes=[sk])
                    S.op('dve', lambda e, r_t=r_t: e.tensor_tensor(out=tmp[:], in0=St[:], in1=bcI(r_t), op=ALU.mult),
                         reads=[sk, jk['r']], writes=[name + '_tmp'])
                    S.op('dve', lambda e, t=t: e.tensor_reduce(out=yv[i][:, t], in_=tmp[:], axis=AX.X, op=ALU.add),
                         reads=[name + '_tmp'], writes=[yk])
                store_y(b, yv[i], yk)
            state_out(St, sk)

        sc_keys_s = lambda nm: ['sc_%s_16' % nm]
        with ExitStack() as p3:
            def s_load_j(qn, b, tile_, key):
                src = bass.AP(tensor=scS[qn].tensor, offset=scS[qn].offset, ap=[[128, 128], [16384, 4], [1, 128]])
                S.dma('sp', tile_[:].rearrange("p t g j -> p t (g j)"), src, reads=sc_keys_s(qn), writes=[key])

            def s_load_v(b, tile_, key):
                src = bass.AP(tensor=scS['v'].tensor, offset=scS['v'].offset, ap=[[128, 128], [16384, 4], [1, 128]])
                S.dma('sp', tile_[:].rearrange("p t g i -> p t (g i)"), src, reads=sc_keys_s('v'), writes=[key])

            def s_store_y(b, tile_, key):
                dst = bass.AP(tensor=yscS.tensor, offset=yscS.offset, ap=[[128, 128], [16384, 4], [1, 128]])
                S.dma('sp', dst, tile_[:].rearrange("p t g i -> p t (g i)"), reads=[key], writes=['yscS'])

            def s_init(St, sk):
                S.dma('sp', St[:].rearrange("p g i j -> p (g i j)"), wkv0[:, :], writes=[sk])

            def s_out(St, sk):
                S.dma('sp', wkv_s[:, :], St[:].rearrange("p g i j -> p (g i j)"), reads=[sk], writes=['o_wkv_s'])
            scan(p3, "ss", 2, 64, 64, 4, 4, s_load_j, s_load_v, s_store_y, s_init, s_out)
        S.barrier()
        with ExitStack() as p3:
            m_su = sbuf(p3, "m_su", [128, 128]); m_sl = sbuf(p3, "m_sl", [128, 128])
            S.op('dve', lambda e: e.tensor_scalar(out=m_su[:], in0=io_fp[:], scalar1=0.0, scalar2=None, op0=ALU.is_gt), reads=['io_fp'], writes=['m_su'])
            S.op('dve', lambda e: e.tensor_scalar(out=m_sl[:], in0=io_fp[:], scalar1=0.0, scalar2=None, op0=ALU.is_lt), reads=['io_fp'], writes=['m_sl'])
            CQ = ["r", "w", "k", "v", "a", "b"]
            IN = [{qn: sbuf(p3, "cin_%s%d" % (qn, i), [128, 1024]) for qn in CQ} for i in range(2)]
            E1 = sbuf(p3, "cE1", [128, 1024]); E2 = sbuf(p3, "cE2", [128, 1024]); E3 = sbuf(p3, "cE3", [128, 1024])
            TT = {qn: sbuf(p3, "cTT_" + qn, [128, 8, 128], BF16) for qn in ("r", "a", "b", "k")}
            PCt = sbuf(p3, "cPC", [128, 8])
            TQb = {qn: sbuf(p3, "cTQb_" + qn, [128, 1024], BF16) for qn in ("r", "a", "b", "k")}
            STb = sbuf(p3, "cSTb", [128, 8, 64], BF16)
            MK = {kn: sbuf(p3, "cM_" + kn, [128, 16, 128]) for kn in ("AbT", "ArbT", "AkT", "ArkT", "Tt")}
            for kn in ("Ab", "AbTb", "Pb", "Qb", "Ttb"):
                MK[kn] = sbuf(p3, "cM_" + kn, [128, 16, 128], BF16)
            Xt = sbuf(p3, "cX", [128, 1024]); Ut = sbuf(p3, "cU", [128, 1024]); Yt = sbuf(p3, "cY", [128, 1024])
            ST = sbuf(p3, "cST", [128, 8, 64])
            S.op('dve', lambda e: e.memset(ST[:], 0.0), writes=['cST'])
            S.op('dve', lambda e: e.memset(STb[:], 0.0), writes=['cSTb'])
            bk = {'n': 0}

            def nbank():
                bk['n'] += 1
                return bk['n'] % 8

            def load_tile(tt):
                i = tt % 2
                for qn in CQ:
                    S.dma('sp', IN[i][qn][:], scP[qn][tt * 128:(tt + 1) * 128, :], reads=['sc_%s_%d' % (qn, tt)], writes=['cin_%s%d' % (qn, i)])
            bc4 = lambda m: m[:].unsqueeze(1).to_broadcast([128, 4, 128])
            CH_STOP = int(os.environ.get('CH_STOP', '9'))
            load_tile(0)
            for tt in range(16):
                i = tt % 2
                if tt + 1 < 16:
                    load_tile(tt + 1)
                X_ = IN[i]
                kx = {qn: 'cin_%s%d' % (qn, i) for qn in CQ}
                if CH_STOP >= 1:
                    for j in range(2):
                        S.op('pe', lambda e, j=j: e.matmul(PS[:, j, :], lhsT=cmask[:], rhs=X_['w'][:, j * 512:(j + 1) * 512], start=True, stop=True),
                             reads=['cmask', kx['w']], writes=['ps%d' % j])
                    for j in range(2):
                        cs = slice(j * 512, (j + 1) * 512)
                        S.op('dve', lambda e, j=j, cs=cs: e.tensor_copy(out=Xt[:, cs], in_=PS[:, j, :]), reads=['ps%d' % j], writes=['cX'])
                    zb_ = eps_tiles[0.0][:, 0:1]
                    S.op('act', lambda e: e.activation(out=E1[:], in_=Xt[:], func=AF.Exp, scale=1.0, bias=zb_), reads=['cX', 'epsc'], writes=['cE1'])
                    S.op('act', lambda e: e.activation(out=E2[:], in_=Xt[:], func=AF.Exp, scale=-1.0, bias=zb_), reads=['cX', 'epsc'], writes=['cE2'])
                    S.op('dve', lambda e: e.tensor_tensor(out=E3[:], in0=Xt[:], in1=X_['w'][:], op=ALU.subtract), reads=['cX', kx['w']], writes=['cE3'])
                    S.op('act', lambda e: e.activation(out=E3[:], in_=E3[:], func=AF.Exp, scale=1.0, bias=eps_tiles[0.0][:, 0:1]), reads=['cE3', 'epsc'], writes=['cE3'])
                    S.op('dve', lambda e: e.tensor_tensor(out=TQb['r'][:], in0=X_['r'][:], in1=E1[:], op=ALU.mult), reads=[kx['r'], 'cE1'], writes=['cTQb_r'])
                    S.op('pool', lambda e: e.tensor_tensor(out=TQb['a'][:], in0=X_['a'][:], in1=E3[:], op=ALU.mult), reads=[kx['a'], 'cE3'], writes=['cTQb_a'])
                    S.op('dve', lambda e: e.tensor_tensor(out=X_['b'][:], in0=X_['b'][:], in1=E2[:], op=ALU.mult), reads=[kx['b'], 'cE2'], writes=[kx['b']])
                    S.op('pool', lambda e: e.tensor_tensor(out=X_['k'][:], in0=X_['k'][:], in1=E2[:], op=ALU.mult), reads=[kx['k'], 'cE2'], writes=[kx['k']])
                    S.op('act', lambda e: e.activation(out=TQb['b'][:], in_=X_['b'][:], func=AF.Copy), reads=[kx['b']], writes=['cTQb_b'])
                    S.op('act', lambda e: e.activation(out=TQb['k'][:], in_=X_['k'][:], func=AF.Copy), reads=[kx['k']], writes=['cTQb_k'])
                if CH_STOP >= 2:
                    for qn in ("r", "a", "b", "k"):
                        bank = nbank()
                        psb_ = PS[:, bank, :].bitcast(BF16)
                        for c in range(8):
                            S.op('pe', lambda e, c=c, qn=qn, psb_=psb_: e.transpose(out=psb_[:, c * 128:(c + 1) * 128], in_=TQb[qn][:, c * 128:(c + 1) * 128], identity=identb[:]),
                                 reads=['cTQb_' + qn, 'identb'], writes=['ps%d' % bank])
                        copy(evac_eng(), TT[qn][:], psb_.rearrange("p (a b) -> p a b", a=8), ['ps%d' % bank], ['cTT_' + qn])
                    bank = nbank()
                    for c in range(8):
                        S.op('pe', lambda e, c=c, bank=bank: e.matmul(PS[:, bank, c:c + 1], lhsT=E1[:, c * 128:(c + 1) * 128], rhs=ident[:, 127:128], start=True, stop=True),
                             reads=['cE1', 'ident'], writes=['ps%d' % bank])
                    copy('dve', PCt[:], PS[:, bank, 0:8], ['ps%d' % bank], ['cPC'])
                hT = lambda qn, h: TT[qn][64 * (h % 2):64 * (h % 2) + 64, h // 2, :]
                if CH_STOP >= 3:
                    for g in range(4):
                        for kn, lq, rq, msk, mkey in (("AbT", "b", "a", m_su, 'm_su'), ("ArbT", "b", "r", cmask, 'cmask'), ("AkT", "k", "a", m_su, 'm_su'),
                                                      ("ArkT", "k", "r", cmask, 'cmask'), ("Ab", "a", "b", m_sl, 'm_sl')):
                            bank = nbank()
                            for hi in range(4):
                                h = 4 * g + hi
                                S.op('pe', lambda e, h=h, hi=hi, bank=bank, lq=lq, rq=rq: e.matmul(PS[:, bank, hi * 128:(hi + 1) * 128], lhsT=hT(lq, h), rhs=hT(rq, h),
                                                                                              start=True, stop=True),
                                     reads=['cTT_' + lq, 'cTT_' + rq], writes=['ps%d' % bank])
                            S.op('dve', lambda e, kn=kn, g=g, bank=bank, msk=msk: e.tensor_tensor(
                                out=MK[kn][:, 4 * g:4 * g + 4, :], in0=PS[:, bank, :].rearrange("p (a b) -> p a b", a=4), in1=bc4(msk), op=ALU.mult),
                                 reads=['ps%d' % bank, mkey], writes=['cM_%s_%d' % (kn, g)])
                if CH_STOP >= 4:
                    for g in range(4):
                        S.op('pool', lambda e, g=g: e.tensor_tensor(out=MK["Tt"][:, 4 * g:4 * g + 4, :], in0=MK["AbT"][:, 4 * g:4 * g + 4, :], in1=bc4(ident), op=ALU.add),
                             reads=['cM_AbT_%d' % g, 'ident'], writes=['cM_Tt_%d' % g])
                        S.op('pool', lambda e, g=g: e.tensor_copy(out=MK["AbTb"][:, 4 * g:4 * g + 4, :], in_=MK["AbT"][:, 4 * g:4 * g + 4, :]),
                             reads=['cM_AbT_%d' % g], writes=['cM_AbTb_%d' % g])
                        S.op('pool', lambda e, g=g: e.tensor_copy(out=MK["Ttb"][:, 4 * g:4 * g + 4, :], in_=MK["Tt"][:, 4 * g:4 * g + 4, :]),
                             reads=['cM_Tt_%d' % g], writes=['cM_Ttb_%d' % g])
                    Pn, Qn, Po, Qo = "Ab", "AbTb", "Pb", "Qb"
                    for lev in range(1, 7):
                        for g in range(4):
                            hs = slice(4 * g, 4 * g + 4)
                            bank = nbank()
                            for hi in range(4):
                                h = 4 * g + hi
                                S.op('pe', lambda e, h=h, hi=hi, bank=bank, Pn=Pn, Qn=Qn: e.matmul(PS[:, bank, hi * 128:(hi + 1) * 128], lhsT=MK[Qn][:, h, :], rhs=MK[Pn][:, h, :],
                                                                                              start=True, stop=True),
                                     reads=['cM_%s_%d' % (Pn, g), 'cM_%s_%d' % (Qn, g)], writes=['ps%d' % bank])
                            copy('act', MK[Po][:, hs, :], PS[:, bank, :].rearrange("p (a b) -> p a b", a=4), ['ps%d' % bank], ['cM_%s_%d' % (Po, g)])
                            if lev < 6:
                                bank = nbank()
                                for hi in range(4):
                                    h = 4 * g + hi
                                    S.op('pe', lambda e, h=h, hi=hi, bank=bank, Pn=Pn, Qn=Qn: e.matmul(PS[:, bank, hi * 128:(hi + 1) * 128], lhsT=MK[Pn][:, h, :], rhs=MK[Qn][:, h, :],
                                                                                                  start=True, stop=True),
                                         reads=['cM_%s_%d' % (Pn, g), 'cM_%s_%d' % (Qn, g)], writes=['ps%d' % bank])
                                copy('act', MK[Qo][:, hs, :], PS[:, bank, :].rearrange("p (a b) -> p a b", a=4), ['ps%d' % bank], ['cM_%s_%d' % (Qo, g)])
                            bank = nbank()
                            for hi in range(4):
                                h = 4 * g + hi
                                S.op('pe', lambda e, h=h, hi=hi, bank=bank, Po=Po: e.matmul(PS[:, bank, hi * 128:(hi + 1) * 128], lhsT=MK[Po][:, h, :], rhs=MK["Ttb"][:, h, :],
                                                                                       start=True, stop=True),
                                     reads=['cM_%s_%d' % (Po, g), 'cM_Ttb_%d' % g], writes=['ps%d' % bank])
                            S.op('dve', lambda e, hs=hs, bank=bank: e.tensor_tensor(out=MK["Tt"][:, hs, :], in0=MK["Tt"][:, hs, :],
                                                                                in1=PS[:, bank, :].rearrange("p (a b) -> p a b", a=4), op=ALU.add),
                                 reads=['ps%d' % bank, 'cM_Tt_%d' % g], writes=['cM_Tt_%d' % g])
                            if lev < 6:
                                S.op('pool', lambda e, hs=hs: e.tensor_copy(out=MK["Ttb"][:, hs, :], in_=MK["Tt"][:, hs, :]),
                                     reads=['cM_Tt_%d' % g], writes=['cM_Ttb_%d' % g])
                        Pn, Qn, Po, Qo = Po, Qo, Pn, Qn
                sth = lambda h: STb[64 * (h % 2):64 * (h % 2) + 64, h // 2, :]
                allg = lambda kn: ['cM_%s_%d' % (kn, g) for g in range(4)]
                if CH_STOP >= 5:
                    a0_ = nbank(); a1_ = nbank(); b0 = nbank(); b1 = nbank()
                    for h in range(16):
                        cs = slice((h % 8) * 64, (h % 8) * 64 + 64)
                        ba = a0_ if h < 8 else a1_
                        bb = b0 if h < 8 else b1
                        S.op('pe', lambda e, h=h, ba=ba, cs=cs: e.matmul(PS[:, ba, cs], lhsT=hT("a", h), rhs=sth(h), start=True, stop=True),
                             reads=['cTT_a', 'cSTb'], writes=['ps%d' % ba])
                        S.op('pe', lambda e, h=h, bb=bb, cs=cs: e.matmul(PS[:, bb, cs], lhsT=MK["AkT"][:, h, :], rhs=X_['v'][:, h * 64:(h + 1) * 64], start=True, stop=True),
                             reads=allg("AkT") + [kx['v']], writes=['ps%d' % bb])
                    copy('act', Xt[:, 0:512], PS[:, a0_, :], ['ps%d' % a0_], ['cX'])
                    copy('act', Xt[:, 512:1024], PS[:, a1_, :], ['ps%d' % a1_], ['cX'])
                    S.op('dve', lambda e: e.tensor_tensor(out=Xt[:, 0:512], in0=Xt[:, 0:512], in1=PS[:, b0, :], op=ALU.add), reads=['cX', 'ps%d' % b0], writes=['cX'])
                    S.op('dve', lambda e: e.tensor_tensor(out=Xt[:, 512:1024], in0=Xt[:, 512:1024], in1=PS[:, b1, :], op=ALU.add), reads=['cX', 'ps%d' % b1], writes=['cX'])
                if CH_STOP >= 6:
                    b0 = nbank(); b1 = nbank()
                    for h in range(16):
                        bank = b0 if h < 8 else b1
                        cs = slice((h % 8) * 64, (h % 8) * 64 + 64)
                        S.op('pe', lambda e, h=h, bank=bank, cs=cs: e.matmul(PS[:, bank, cs], lhsT=MK["Tt"][:, h, :], rhs=Xt[:, h * 64:(h + 1) * 64], start=True, stop=True),
                             reads=allg("Tt") + ['cX'], writes=['ps%d' % bank])
                    copy('act', Ut[:, 0:512], PS[:, b0, :], ['ps%d' % b0], ['cU'])
                    copy('dve', Ut[:, 512:1024], PS[:, b1, :], ['ps%d' % b1], ['cU'])
                if CH_STOP >= 7:
                    a0_ = nbank(); a1_ = nbank(); b0 = nbank(); b1 = nbank()
                    for h in range(16):
                        cs = slice((h % 8) * 64, (h % 8) * 64 + 64)
                        ba = a0_ if h < 8 else a1_
                        bb = b0 if h < 8 else b1
                        S.op('pe', lambda e, h=h, ba=ba, cs=cs: e.matmul(PS[:, ba, cs], lhsT=hT("r", h), rhs=sth(h), start=True, stop=True),
                             reads=['cTT_r', 'cSTb'], writes=['ps%d' % ba])
                        S.op('pe', lambda e, h=h, bb=bb, cs=cs: e.matmul(PS[:, bb, cs], lhsT=MK["ArbT"][:, h, :], rhs=Ut[:, h * 64:(h + 1) * 64], start=True, stop=False),
                             reads=allg("ArbT") + ['cU'], writes=['ps%d' % bb])
                        S.op('pe', lambda e, h=h, bb=bb, cs=cs: e.matmul(PS[:, bb, cs], lhsT=MK["ArkT"][:, h, :], rhs=X_['v'][:, h * 64:(h + 1) * 64], start=False, stop=True),
                             reads=allg("ArkT") + [kx['v']], writes=['ps%d' % bb])
                    if tt >= 8:
                        copy('act', Yt[:, 0:512], PS[:, a0_, :], ['ps%d' % a0_], ['cY'])
                        copy('act', Yt[:, 512:1024], PS[:, a1_, :], ['ps%d' % a1_], ['cY'])
                        S.op('dve', lambda e: e.tensor_tensor(out=Yt[:, 0:512], in0=Yt[:, 0:512], in1=PS[:, b0, :], op=ALU.add), reads=['cY', 'ps%d' % b0], writes=['cY'])
                        S.op('dve', lambda e: e.tensor_tensor(out=Yt[:, 512:1024], in0=Yt[:, 512:1024], in1=PS[:, b1, :], op=ALU.add), reads=['cY', 'ps%d' % b1], writes=['cY'])
                        S.dma('sp', yscP[tt * 128:(tt + 1) * 128, :], Yt[:], reads=['cY'], writes=['yscP%d' % tt])
                if CH_STOP < 7 and tt >= 8:
                    for j in range(2):
                        S.dma('sp', yscP[tt * 128:(tt + 1) * 128, j * 512:(j + 1) * 512], zero_t[:, :], reads=['zero_t'], writes=['yscP%d' % tt])
                if CH_STOP >= 8:
                    sb0 = nbank()
                    sb1 = (sb0 + 1) % 8
                    bk['n'] += 1
                    for hp in range(8):
                        bank = sb0 if hp < 4 else sb1
                        cs = slice((hp % 4) * 128, (hp % 4) * 128 + 128)
                        ps_ = slice(hp * 128, (hp + 1) * 128)
                        S.op('pe', lambda e, bank=bank, cs=cs, ps_=ps_: e.matmul(PS[:, bank, cs], lhsT=X_['b'][:, ps_], rhs=Ut[:, ps_], start=True, stop=False),
                             reads=[kx['b'], 'cU'], writes=['ps%d' % bank])
                        S.op('pe', lambda e, bank=bank, cs=cs, ps_=ps_: e.matmul(PS[:, bank, cs], lhsT=X_['k'][:, ps_], rhs=X_['v'][:, ps_], start=False, stop=True),
                             reads=[kx['k'], kx['v']], writes=['ps%d' % bank])
                    for h2 in range(2):
                        pr = slice(64 * h2, 64 * h2 + 64)
                        for bank, hp0 in ((sb0, 0), (sb1, 4)):
                            psv = PS[pr, bank, :].rearrange("p (a x) -> p a x", a=4)[:, :, 64 * h2:64 * h2 + 64]
                            S.op('dve', lambda e, pr=pr, psv=psv, hp0=hp0: e.tensor_tensor(out=ST[pr, hp0:hp0 + 4, :], in0=ST[pr, hp0:hp0 + 4, :], in1=psv, op=ALU.add),
                                 reads=['cST', 'ps%d' % bank], writes=['cST'])
                        S.op('dve', lambda e, pr=pr: e.tensor_tensor(out=ST[pr, :, :], in0=ST[pr, :, :], in1=PCt[pr, :].unsqueeze(2).to_broadcast([64, 8, 64]), op=ALU.mult),
                             reads=['cST', 'cPC'], writes=['cST'])
                    S.op('pool', lambda e: e.tensor_copy(out=STb[:], in_=ST[:]), reads=['cST'], writes=['cSTb'])
            WO = sbuf(p3, "cWO", [64, 8, 128])
            for hp in range(8):
                bank = 0 if hp < 4 else 1
                S.op('pe', lambda e, hp=hp, bank=bank: e.transpose(out=PS[0:64, bank, (hp % 4) * 128:(hp % 4 + 1) * 128], in_=ST[:, hp, :], identity=ident[:]),
                     reads=['cST', 'ident'], writes=['ps%d' % bank])
            copy('dve', WO[:, 0:4, :], PS[0:64, 0, :].rearrange("p (a b) -> p a b", a=4), ['ps0'], ['cWO'])
            copy('dve', WO[:, 4:8, :], PS[0:64, 1, :].rearrange("p (a b) -> p a b", a=4), ['ps1'], ['cWO'])
            for h2 in range(2):
                dst = bass.AP(tensor=wkv_p.tensor, offset=wkv_p.offset + h2 * 4096, ap=[[64, 64], [8192, 8], [1, 64]])
                S.dma('sp', dst, WO[:, :, h2 * 64:(h2 + 1) * 64], reads=['cWO'], writes=['o_wkv_p%d' % h2])

        S.barrier()
        with ExitStack() as p4:
            g2 = sbuf(p4, "g2", [128, D]); gf = sbuf(p4, "gf", [128, D])
            gng = sbuf(p4, "gng", [128, 1024]); gnb = sbuf(p4, "gnb", [128, 1024])
            S.dma('sp', g2[:], bc_rows(norm2_g, D), writes=['g2'])
            S.dma('sp', gf[:], bc_rows(final_g, D), writes=['gf'])
            S.dma('sp', gng[:], bc_rows(gn_g, 1024), writes=['gng'])
            S.dma('sp', gnb[:], bc_rows(gn_b, 1024), writes=['gnb'])
            skt = sbuf(p4, "skt", [128, 2, 128])
            S.dma('sp', skt[:], skT[:, :, :], writes=['skt'])
            Hs = [sbuf(p4, "H%d" % i, [128, D]) for i in range(2)]
            B1 = sbuf(p4, "B1", [128, D])
            B2 = sbuf(p4, "B2", [128, D])
            B3 = sbuf(p4, "B3", [128, D])
            TB_ = sbuf(p4, "TB_", [128, 16, 128], BF16)
            QT = sbuf(p4, "QT", [128, 16, 128])
            xn2b = sbuf(p4, "xn2b", [128, D], BF16)
            wb = [sbuf(p4, "wb%d" % i, [128, 16, 512], BF16) for i in range(2)]
            NU = 6
            U = [sbuf(p4, "U%d" % i, [128, D]) for i in range(NU)]
            Vb = [sbuf(p4, "Vb%d" % i, [128, D], BF16) for i in range(2)]
            OH = sbuf(p4, "OH", [128, 4, 256])
            top = sbuf(p4, "top", [128, 16, 16]); tidx = sbuf(p4, "tidx", [128, 16, 16]); tiu = sbuf(p4, "tiu", [128, 16], U32)
            wk = sbuf(p4, "wk", [128, 256])
            bsv = sbuf(p4, "bsv", [128, 8, 16]); bidx = sbuf(p4, "bidx", [128, 8, 16]); eid = sbuf(p4, "eid", [128, 128])
            gat = sbuf(p4, "gat", [128, 128]); zz = sbuf(p4, "zz", [128, 8]); nmx = sbuf(p4, "nmx", [128, 8])
            eidTs = [sbuf(p4, "eidT%d" % i, [128, 128], I32) for i in range(2)]; gatTs = [sbuf(p4, "gatT%d" % i, [128, 128]) for i in range(2)]
            actTs = [sbuf(p4, "actT%d" % i, [128, 128]) for i in range(2)]
            Ln = [sbuf(p4, "Ln%d" % i, [128, 128], BF16) for i in range(4)]
            idc = [sbuf(p4, "idc%d" % i, [128, 1], I32) for i in range(4)]
            ss = sbuf(p4, "ss", [128, 4]); st16 = sbuf(p4, "st16", [128, 16]); st16b = sbuf(p4, "st16b", [128, 16])
            v3 = lambda ap: ap.rearrange("p (h j) -> p h j", h=16)
            bc16 = lambda t: t[:, 0:16].unsqueeze(2).to_broadcast([128, 16, 64])
            wload = {'n': 0}

            def big_mm(wsrc, dst_fn, post):
                for cbk in range(4):
                    i = wload['n'] % 2
                    wload['n'] += 1
                    w = wb[i]; wkey = 'wb%d' % i
                    S.dma('pool', w[:], wsrc.rearrange("(a p) c -> p a c", p=128)[:, :, cbk * 512:(cbk + 1) * 512], writes=[wkey])
                    bank = 2 + (cbk % 2)
                    for dc in range(16):
                        S.op('pe', lambda e, dc=dc, bank=bank, w=w: e.matmul(PS[:, bank, :], lhsT=TB_[:, dc, :], rhs=w[:, dc, :],
                                                                         start=(dc == 0), stop=(dc == 15)),
                             reads=['TB_', wkey], writes=['ps%d' % bank])
                    post(cbk, bank)

            def tile_front(ot):
                samp = ot == 8
                nr = NS if samp else 128
                orow = ot * 128
                tt = ot + 8
                H = Hs[ot % 2]; hk = 'H%d' % (ot % 2)
                eidT = eidTs[ot % 2]; ek = 'eidT%d' % (ot % 2)
                gatT = gatTs[ot % 2]; gk = 'gatT%d' % (ot % 2)
                actT = actTs[ot % 2]; ak = 'actT%d' % (ot % 2)
                cat = B1
                S.op('pool', lambda e: e.memset(B2[:], 0.0), writes=['B2'])
                S.op('pool', lambda e: e.memset(cat[:], 0.0), writes=['B1'])
                S.op('pool', lambda e: e.memset(H[:], 0.0), writes=[hk])
                yt = B2[:, 0:1024]; stg2 = B2[:, 1024:2048]
                if samp:
                    S.dma('sp', B2[0:NS, 0:1024], yscS[:, :], reads=['yscS'], writes=['B2'])
                    S.dma('sp', H[0:NS, :], xs[:, :], writes=[hk])
                else:
                    S.dma('sp', yt, yscP[NT - NOWN + orow:NT - NOWN + orow + 128, :], reads=['yscP%d' % tt], writes=['B2'])
                    S.dma('sp', H[:], xseq[NT - NOWN + orow:NT - NOWN + orow + 128, :], writes=[hk])
                S.dma('sp', cat[0:nr, 0:1024], yaP[orow:orow + nr, :], reads=['yaP%d' % tt], writes=['B1'])
                S.op('dve', lambda e: e.tensor_reduce(out=st16[:], in_=v3(yt), axis=AX.X, op=ALU.add), reads=['B2'], writes=['st16'])
                S.op('dve', lambda e: e.tensor_scalar(out=st16[:], in0=st16[:], scalar1=-1.0 / 64, scalar2=None, op0=ALU.mult), reads=['st16'], writes=['st16'])
                S.op('dve', lambda e: e.tensor_tensor(out=v3(yt), in0=v3(yt), in1=bc16(st16), op=ALU.add), reads=['B2', 'st16'], writes=['B2'])
                S.op('dve', lambda e: e.tensor_tensor(out=stg2, in0=yt, in1=yt, op=ALU.mult), reads=['B2'], writes=['B2s'])
                S.op('dve', lambda e: e.tensor_reduce(out=st16b[:], in_=v3(stg2), axis=AX.X, op=ALU.add), reads=['B2s'], writes=['st16b'])
                rstd_from_ss(st16b[:], 64.0, GN_EPS, 'st16b')
                S.op('dve', lambda e: e.tensor_tensor(out=v3(yt), in0=v3(yt), in1=bc16(st16b), op=ALU.mult), reads=['B2', 'st16b'], writes=['B2'])
                S.op('dve', lambda e: e.tensor_tensor(out=yt, in0=yt, in1=gng[:], op=ALU.mult), reads=['B2', 'gng'], writes=['B2'])
                S.op('dve', lambda e: e.tensor_tensor(out=yt, in0=yt, in1=gnb[:], op=ALU.add), reads=['B2', 'gnb'], writes=['B2'])
                S.dma('sp', B2[0:nr, 1024:2048], bonP[orow:orow + nr, :], reads=['bonP%d' % tt, 'B2s'], writes=['B2s'])
                S.op('dve', lambda e: e.tensor_tensor(out=yt, in0=yt, in1=stg2, op=ALU.add), reads=['B2', 'B2s'], writes=['B2'])
                S.dma('sp', B2[0:nr, 1024:2048], gP[orow:orow + nr, :], reads=['gP%d' % tt, 'B2', 'B2s'], writes=['B2s'])
                S.op('dve', lambda e: e.tensor_tensor(out=cat[:, 1024:2048], in0=yt, in1=stg2, op=ALU.mult), reads=['B2', 'B2s', 'B1'], writes=['B1'])
                transpose_to(lambda c0, nb: TB_[:, c0:c0 + nb, :], cat, 16, 'B1', 'TB_')
                big_mm(w_out, None, lambda cbk, bank: S.op('dve', lambda e: e.tensor_tensor(
                    out=H[:, cbk * 512:(cbk + 1) * 512], in0=PS[:, bank, :], in1=H[:, cbk * 512:(cbk + 1) * 512], op=ALU.add),
                    reads=['ps%d' % bank, hk], writes=[hk]))
                xn2 = B1
                rmsnorm(H, g2, hk, 'g2', ss, B2, outt=xn2, okey='B1', jkey='B2')
                S.op('act', lambda e: e.activation(out=xn2b[:], in_=xn2[:], func=AF.Copy), reads=['B1'], writes=['xn2b'])
                transpose_to(lambda c0, nb: TB_[:, c0:c0 + nb, :], xn2, 16, 'B1', 'TB_')
                qt = B2
                big_mm(w_q, None, lambda cbk, bank: copy(evac_eng(), qt[:, cbk * 512:(cbk + 1) * 512], PS[:, bank, :], ['ps%d' % bank], ['B2']))
                transpose_to(lambda c0, nb: QT[:, c0:c0 + nb, :], qt, 16, 'B2', 'QT')
                sc = B3
                for hc in range(16):
                    bank = hc // 4
                    S.op('pe', lambda e, hc=hc, bank=bank: e.matmul(PS[:, bank, (hc % 4) * 128:(hc % 4 + 1) * 128], lhsT=QT[:, hc, :],
                                                                   rhs=skt[:, hc % 2, :], start=True, stop=True),
                         reads=['QT', 'skt'], writes=['ps%d' % bank])
                for b4 in range(4):
                    copy(evac_eng(), sc[:, b4 * 512:(b4 + 1) * 512], PS[:, b4, :], ['ps%d' % b4], ['B3'])

                def top16(src_ap, n, vals, idxf, vkey, ikey, skey):
                    cur = src_ap
                    for half in range(2):
                        S.op('dve', lambda e, cur=cur, half=half: e.max(out=vals[:, half * 8:(half + 1) * 8], in_=cur), reads=[skey, 'wk'], writes=[vkey])
                        S.op('dve', lambda e, cur=cur, half=half: e.max_index(out=tiu[:, half * 8:(half + 1) * 8], in_max=vals[:, half * 8:(half + 1) * 8], in_values=cur),
                             reads=[skey, 'wk', vkey], writes=['tiu'])
                        if half == 0:
                            S.op('dve', lambda e, cur=cur: e.match_replace(out=wk[:, 0:n], in_to_replace=vals[:, 0:8], in_values=cur, imm_value=-1e30),
                                 reads=[skey, vkey], writes=['wk'])
                            cur = wk[:, 0:n]
                    S.op('dve', lambda e: e.tensor_copy(out=idxf, in_=tiu[:]), reads=['tiu'], writes=[ikey])

                for hc in range(16):
                    top16(sc[:, hc * 128:(hc + 1) * 128], 128, top[:, hc, :], tidx[:, hc, :], 'top', 'tidx', 'B3')
                cand = B2; cid = B3
                t4 = top[:].rearrange("p (h c) k -> p h c k", c=2)
                i4 = tidx[:].rearrange("p (h c) k -> p h c k", c=2)
                c4 = lambda t_: t_[:].rearrange("p (h a b) -> p h a b", h=8, a=16)
                S.op('dve', lambda e: e.tensor_tensor(out=c4(cand), in0=t4[:, :, 0, :].unsqueeze(3).to_broadcast([128, 8, 16, 16]),
                                                      in1=t4[:, :, 1, :].unsqueeze(2).to_broadcast([128, 8, 16, 16]), op=ALU.add),
                     reads=['top'], writes=['B2'])
                for h in range(8):
                    S.op('dve', lambda e, h=h: e.scalar_tensor_tensor(
                        out=cid[:, h * 256:(h + 1) * 256].rearrange("p (a b) -> p a b", a=16),
                        in0=tidx[:, 2 * h, :].unsqueeze(2).to_broadcast([128, 16, 16]), scalar=128.0,
                        in1=tidx[:, 2 * h + 1, :].unsqueeze(1).to_broadcast([128, 16, 16]), op0=ALU.mult, op1=ALU.add),
                         reads=['tidx'], writes=['B3'])
                for h in range(8):
                    top16(cand[:, h * 256:(h + 1) * 256], 256, bsv[:, h, :], bidx[:, h, :], 'bsv', 'bidx', 'B2')
                    for k0 in (0, 4, 8, 12):
                        S.op('dve', lambda e, h=h, k0=k0: e.tensor_tensor(out=OH[:], in0=io_f[:, 0:256].unsqueeze(1).to_broadcast([128, 4, 256]),
                                                                   in1=bidx[:, h, k0:k0 + 4].unsqueeze(2).to_broadcast([128, 4, 256]), op=ALU.is_equal),
                             reads=['io_f', 'bidx'], writes=['OH'])
                        S.op('dve', lambda e, h=h: e.tensor_tensor(out=OH[:], in0=OH[:], in1=cid[:, h * 256:(h + 1) * 256].unsqueeze(1).to_broadcast([128, 4, 256]),
                                                                   op=ALU.mult), reads=['OH', 'B3'], writes=['OH'])
                        S.op('dve', lambda e, h=h, k0=k0: e.tensor_reduce(out=eid[:, h * 16 + k0:h * 16 + k0 + 4], in_=OH[:], axis=AX.X, op=ALU.add), reads=['OH'], writes=['eid'])
                S.op('dve', lambda e: e.tensor_scalar(out=nmx[:], in0=bsv[:, :, 0], scalar1=-1.0, scalar2=None, op0=ALU.mult), reads=['bsv'], writes=['nmx'])
                S.op('dve', lambda e: e.memset(zz[:], 0.0), writes=['zz'])
                for h in range(8):
                    S.op('act', lambda e, h=h: e.activation(out=gat[:, h * 16:(h + 1) * 16], in_=bsv[:, h, :], func=AF.Exp, bias=nmx[:, h:h + 1], scale=1.0,
                                                            accum_out=zz[:, h:h + 1]), reads=['bsv', 'nmx', 'zz'], writes=['gat', 'zz'])
                S.op('dve', lambda e: e.reciprocal(out=zz[:], in_=zz[:]), reads=['zz'], writes=['zz'])
                S.op('dve', lambda e: e.tensor_tensor(out=gat[:].rearrange("p (h k) -> p h k", h=8), in0=gat[:].rearrange("p (h k) -> p h k", h=8),
                                                      in1=zz[:].unsqueeze(2).to_broadcast([128, 8, 16]), op=ALU.mult), reads=['gat', 'zz'], writes=['gat'])
                S.op('pe', lambda e: e.transpose(out=PS[:, 0, 0:128], in_=eid[:], identity=ident[:]), reads=['eid', 'ident'], writes=['ps0'])
                S.op('pe', lambda e: e.transpose(out=PS[:, 0, 128:256], in_=gat[:], identity=ident[:]), reads=['gat', 'ident'], writes=['ps0'])
                S.op('dve', lambda e: e.tensor_copy(out=eidT[:], in_=PS[:, 0, 0:128]), reads=['ps0'], writes=[ek])
                S.op('dve', lambda e: e.tensor_copy(out=gatT[:], in_=PS[:, 0, 128:256]), reads=['ps0'], writes=[gk])
                S.op('dve', lambda e: e.memset(actT[:], 0.0), writes=[ak])
            def tile_p1(ot):
                samp = ot == 8
                nr = NS if samp else 128
                orow = ot * 128
                tt = ot + 8
                H = Hs[ot % 2]; hk = 'H%d' % (ot % 2)
                eidT = eidTs[ot % 2]; ek = 'eidT%d' % (ot % 2)
                gatT = gatTs[ot % 2]; gk = 'gatT%d' % (ot % 2)
                actT = actTs[ot % 2]; ak = 'actT%d' % (ot % 2)
                junk = B2
                for n in range(nr):
                    i = n % NU
                    S.gather(U[i][:, :], u_tab[:, :], eidT[:, n:n + 1], NEXP, reads=[ek], writes=['U%d' % i])
                    sel = identb[:, n:n + 1].to_broadcast([128, 128])
                    pb = 4 * (n % 2)
                    for j in range(4):
                        S.op('pe', lambda e, j=j, sel=sel, pb=pb: e.matmul(PS[:, pb + j, :], lhsT=sel, rhs=xn2b[:, j * 512:(j + 1) * 512], start=True, stop=True),
                             reads=['xn2b', 'identb'], writes=['ps%d' % (pb + j)])
                    S.op('dve', lambda e, i=i, n=n, pb=pb: e.scalar_tensor_tensor(out=junk[:], in0=U[i][:], scalar=1.0,
                                                                         in1=PS[:, pb:pb + 4, :].rearrange("p a b -> p (a b)"), op0=ALU.mult, op1=ALU.mult,
                                                                         accum_out=actT[:, n:n + 1]),
                         reads=['U%d' % i] + ['ps%d' % (pb + j) for j in range(4)], writes=['B2', ak])
                S.op('act', lambda e: e.activation(out=actT[:], in_=actT[:], func=AF.Gelu), reads=[ak], writes=[ak])
                S.op('dve', lambda e: e.tensor_tensor(out=actT[:], in0=actT[:], in1=gatT[:], op=ALU.mult), reads=[ak, gk], writes=[ak])
            def tile_p2(ot, deferred):
                samp = ot == 8
                nr = NS if samp else 128
                orow = ot * 128
                tt = ot + 8
                H = Hs[ot % 2]; hk = 'H%d' % (ot % 2)
                eidT = eidTs[ot % 2]; ek = 'eidT%d' % (ot % 2)
                gatT = gatTs[ot % 2]; gk = 'gatT%d' % (ot % 2)
                actT = actTs[ot % 2]; ak = 'actT%d' % (ot % 2)
                kper = (len(deferred) + nr - 1) // nr + 1
                for n in range(nr):
                    i = n % NU
                    S.gather(U[i][:, :], v_tab[:, :], eidT[:, n:n + 1], NEXP, reads=[ek], writes=['U%d' % i])
                    vb_ = Vb[n % 2]; vbk = 'Vb%d' % (n % 2)
                    S.op('act', lambda e, i=i, vb_=vb_: e.activation(out=vb_[:], in_=U[i][:], func=AF.Copy), reads=['U%d' % i], writes=[vbk])
                    L = Ln[n % 4]; lk = 'Ln%d' % (n % 4)
                    S.op('dve', lambda e, L=L, n=n: e.scalar_tensor_tensor(out=L[:], in0=io_f[:, 0:128], scalar=float(n), in1=actT[:],
                                                                         op0=ALU.is_equal, op1=ALU.mult), reads=['io_f', ak], writes=[lk])
                    for j in range(4):
                        S.op('pe', lambda e, j=j, L=L, vb_=vb_, n=n: e.matmul(PS[:, 4 + j, :], lhsT=L[:], rhs=vb_[:, j * 512:(j + 1) * 512],
                                                                     start=(n == 0), stop=(n == nr - 1)),
                             reads=[lk, vbk], writes=['ps%d' % (4 + j)])
                    for _ in range(kper):
                        if deferred:
                            S.replay(deferred.pop(0))
                while deferred:
                    S.replay(deferred.pop(0))
                S.op('dve', lambda e: e.tensor_tensor(out=H[:], in0=H[:], in1=PS[:, 4:8, :].rearrange("p a b -> p (a b)"), op=ALU.add),
                     reads=[hk, 'ps4', 'ps5', 'ps6', 'ps7'], writes=[hk])
                rmsnorm(H, gf, hk, 'gf', ss, B2, outt=B1, okey='B1', jkey='B2')
                if samp:
                    S.dma('sp', y_s[:, :], B1[0:NS, :], reads=['B1'], writes=['o_y_s'])
                else:
                    S.dma('sp', y_p[orow:orow + 128, :], B1[:], reads=['B1'], writes=['o_y_p%d' % ot])
            tile_front(0)
            for ot in range(9):
                tile_p1(ot)
                deferred = []
                if ot + 1 < 9:
                    S.rec = deferred
                    tile_front(ot + 1)
                    S.rec = None
                tile_p2(ot, deferred)
        S.finish('sp')
    return nc


_OUT_NAMES = ["y_p", "y_s", "wkv_p", "shift_p", "chunkv_p", "wkv_s", "shift_s", "chunkv_s"]


def make_in_maps(inp):
    f = lambda a: np.ascontiguousarray(np.asarray(a), dtype=np.float32)
    x_prompt = f(inp["x_prompt"]); x_sample = f(inp["x_sample"])
    ws = f(inp["ws"])[0]; bs = f(inp["bs"])[0]
    wsT = np.ascontiguousarray(ws.transpose(2, 0, 1))
    bsT = np.ascontiguousarray(bs.T)
    wsTs = np.zeros((128, 8, 128), np.float32)
    for s in range(4):
        for t in range(4):
            for b in range(16):
                wsTs[s * 16 + b, :, t * 16 + b] = ws[:, t, s]
    bsTs = np.zeros((128, 8), np.float32)
    bsTs[:64] = np.repeat(bs[:, :4].T, 16, axis=0)
    shared = {
        "norm1_g": f(inp["norm1_g"])[0], "norm2_g": f(inp["norm2_g"])[0], "final_g": f(inp["final_g"]),
        "w_in": f(inp["w_in"])[0], "w_out": f(inp["w_out"])[0], "w_q": f(inp["w_q"])[0],
        "ln_g": f(inp["ln_v_g"])[0], "ln_b": f(inp["ln_v_b"])[0],
        "wsT": wsT, "bsT": bsT, "wsTs": wsTs, "bsTs": bsTs,
        "mu": f(inp["mu"])[0], "w0": f(inp["w0"])[0], "a0": f(inp["a0"])[0],
        "wa_up": np.ascontiguousarray(np.concatenate([f(inp["w_up"])[0], f(inp["a_up"])[0]], axis=0)),
        "g_up": f(inp["g_up"])[0],
        "k_k": f(inp["k_k"])[0], "k_a": f(inp["k_a"])[0], "r_k": f(inp["r_k"])[0].reshape(1024),
        "gn_g": f(inp["gn_g"])[0], "gn_b": f(inp["gn_b"])[0],
        "skT": np.ascontiguousarray(f(inp["sub_keys"])[0].transpose(2, 0, 1)),
        "u_tab": f(inp["u_tab"])[0], "v_tab": f(inp["v_tab"])[0],
    }
    state_wkv = f(inp["state_wkv"])[0]; state_shift = f(inp["state_shift"])[0]
    maps = []
    for c in range(8):
        b, half = c // 2, c % 2
        if half == 1:
            xseq = x_prompt[b]
        else:
            xseq = np.concatenate([np.zeros((1024, D), np.float32), x_prompt[b, :1024]], axis=0)
        xs = x_sample[16 * c:16 * c + 16].transpose(1, 0, 2).reshape(64, D)
        m = dict(shared)
        m["xseq"] = np.ascontiguousarray(xseq)
        m["xs"] = np.ascontiguousarray(xs)
        m["wkv0"] = np.ascontiguousarray(state_wkv[16 * c:16 * c + 16].reshape(128, 8192))
        m["shift0"] = np.ascontiguousarray(state_shift[16 * c:16 * c + 16])
        maps.append(m)
    return maps


def assemble(results):
    y_prompt = np.zeros((4, 2048, D), np.float32)
    y_sample = np.zeros((128, 4, D), np.float32)
    wkv_p = np.zeros((1, 4, 16, 64, 64), np.float32)
    shift_p = np.zeros((1, 4, D), np.float32)
    chunkv_p = np.zeros((1, 4, 128, 1024), np.float32)
    wkv_s = np.zeros((1, 128, 16, 64, 64), np.float32)
    shift_s = np.zeros((1, 128, D), np.float32)
    chunkv_s = np.zeros((1, 128, 4, 1024), np.float32)
    for c in range(8):
        r = results[c]
        b, half = c // 2, c % 2
        y_prompt[b, half * 1024:(half + 1) * 1024] = r["y_p"]
        y_sample[16 * c:16 * c + 16] = r["y_s"].reshape(4, 16, D).transpose(1, 0, 2)
        if half == 1:
            wkv_p[0, b] = r["wkv_p"].reshape(16, 64, 64)
            shift_p[0, b] = r["shift_p"][0]
            chunkv_p[0, b] = r["chunkv_p"]
        wkv_s[0, 16 * c:16 * c + 16] = r["wkv_s"].reshape(16, 16, 64, 64)
        shift_s[0, 16 * c:16 * c + 16] = r["shift_s"]
        chunkv_s[0, 16 * c:16 * c + 16] = r["chunkv_s"].reshape(4, 16, 1024).transpose(1, 0, 2)
    return (y_prompt, y_sample, wkv_p, shift_p, chunkv_p, wkv_s, shift_s, chunkv_s)


def kernel(**inputs):
    nc = build()
    maps = make_in_maps(inputs)
    res = run_bass_kernel_spmd(nc, maps, core_ids=list(range(8)))
    return assemble(res.results)
```
